# Optimizing a Trainium2 kernel written in Bass

```python
import math
import jax, jax.numpy as jnp
from jax import lax
import numpy as np

D_MODEL = 1024
BATCH = 32
SEQ = 2048
DEPTH = 2

GRID_W = 64
CTX_LEN = 256

SSD_HEADS = 16
SSD_HEAD_DIM = 64
SSD_INNER = SSD_HEADS * SSD_HEAD_DIM
SSD_GROUPS = 2
SSD_STATE = 128
SSD_CONV = 5
SSD_CHUNK = 128
SSD_XBC = SSD_INNER + 2 * SSD_GROUPS * SSD_STATE
SSD_COLS = SSD_XBC + 2 * SSD_HEADS
FFT_GROUPS = 4
FFT_GROUP_DIM = 128
FFT_WIDTH = FFT_GROUPS * FFT_GROUP_DIM
GMLP_GROUPS = 4
GMLP_GROUP_DIM = 128
GMLP_WIDTH = GMLP_GROUPS * GMLP_GROUP_DIM
GMLP_CHUNK = 128
N_BRANCH = 3
D_FF = 2816
FFN_CONV = 3
EPS = 1e-6

OFF_DT = SSD_XBC
OFF_Z = SSD_COLS
OFF_FFT = OFF_Z + SSD_INNER
OFF_GMLP = OFF_FFT + FFT_WIDTH
OFF_GATE = OFF_GMLP + 2 * GMLP_WIDTH
IN_COLS = OFF_GATE + N_BRANCH * D_MODEL
IN_SPLIT = (OFF_DT, OFF_Z, OFF_FFT, OFF_GMLP, OFF_GATE)

kernel_name = "hybrid_ssd_fnet_gmlp_convffn_dit"


def rmsnorm(x, w):
    xf = x.astype(jnp.float32)
    xf = xf * lax.rsqrt(jnp.mean(xf * xf, axis=-1, keepdims=True) + EPS)
    return (xf * w.astype(jnp.float32)).astype(x.dtype)


def modulate(h, shift, scale):
    return h * (1 + scale) + shift


def dwconv(x, w, b):
    k = w.shape[0]
    y = lax.conv_general_dilated(x, w[:, None, :].astype(x.dtype), window_strides=(1,),
                                 padding=[(k // 2, k // 2)],
                                 dimension_numbers=('NWC', 'WIO', 'NWC'),
                                 feature_group_count=x.shape[-1])
    return y + b


def ssd_scan(x, dt, a, bm, cm, init_state, with_y):
    b, L, H, P = x.shape
    G, N = bm.shape[2], bm.shape[3]
    R = H // G
    Q = SSD_CHUNK
    nc = L // Q
    xf = (x.astype(jnp.float32) * dt[..., None]).reshape(b, nc, Q, G, R, P)
    bf = bm.astype(jnp.float32).reshape(b, nc, Q, G, N)
    la = jnp.transpose((dt * a).reshape(b, nc, Q, G, R), (0, 3, 4, 1, 2))
    cum = jnp.cumsum(la, axis=-1)
    decay_to_end = jnp.exp(cum[..., -1:] - cum)
    states = jnp.einsum('bcqgn,bgrcq,bcqgrp->bcgrpn', bf, decay_to_end, xf)
    cpad = jnp.concatenate([jnp.zeros((b, G, R, 1), jnp.float32),
                            jnp.cumsum(cum[..., -1], axis=-1)], axis=-1)
    tri_c = jnp.tril(jnp.ones((nc + 1, nc + 1), dtype=bool))
    dchunk = jnp.exp(jnp.where(tri_c, cpad[..., :, None] - cpad[..., None, :], -jnp.inf))
    states_all = jnp.concatenate([init_state[:, None], states], axis=1)
    new_states = jnp.einsum('bgrzc,bcgrpn->bzgrpn', dchunk, states_all)
    final = new_states[:, -1]
    if not with_y:
        return None, final
    prev = new_states[:, :-1]
    cf = cm.astype(jnp.float32).reshape(b, nc, Q, G, N)
    tri_q = jnp.tril(jnp.ones((Q, Q), dtype=bool))
    lmat = jnp.exp(jnp.where(tri_q, cum[..., :, None] - cum[..., None, :], -jnp.inf))
    cb = jnp.einsum('bcqgn,bcsgn->bcgqs', cf, bf)
    y_diag = jnp.einsum('bcgqs,bgrcqs,bcsgrp->bcqgrp', cb, lmat, xf)
    y_off = jnp.einsum('bcqgn,bcgrpn,bgrcq->bcqgrp', cf, prev, jnp.exp(cum))
    y = (y_diag + y_off).reshape(b, L, H, P)
    return y.astype(x.dtype), final


def ssd_prep(xbc_raw, dt_raw, conv_w, conv_b, dt_bias):
    b, L, _ = xbc_raw.shape
    xbc = jax.nn.silu(dwconv(xbc_raw, conv_w, conv_b))
    xs, bm, cm = jnp.split(xbc, [SSD_INNER, SSD_INNER + SSD_GROUPS * SSD_STATE], axis=-1)
    xs = xs.reshape(b, L, SSD_HEADS, SSD_HEAD_DIM)
    bm = bm.reshape(b, L, SSD_GROUPS, SSD_STATE)
    cm = cm.reshape(b, L, SSD_GROUPS, SSD_STATE)
    dt = jax.nn.softplus(dt_raw.astype(jnp.float32).reshape(b, L, 2, SSD_HEADS)
                         + dt_bias.astype(jnp.float32))
    return xs, bm, cm, dt


def ssd_bidir(xs, bm, cm, dt, a_log, init_f, init_b, with_y):
    a = -jnp.exp(a_log.astype(jnp.float32))
    flip = lambda t: jnp.flip(t, axis=1)
    y_f, s_f = ssd_scan(xs, dt[:, :, 0], a[0], bm, cm, init_f, with_y)
    y_b, s_b = ssd_scan(flip(xs), flip(dt[:, :, 1]), a[1], flip(bm), flip(cm), init_b, with_y)
    if not with_y:
        return None, s_f, s_b
    return y_f + flip(y_b), s_f, s_b


def ssd_output(y, xs, z, d_skip, norm_w):
    b, L = z.shape[0], z.shape[1]
    y = (y + d_skip[:, None] * xs).reshape(b, L, SSD_INNER) * jax.nn.silu(z)
    y = rmsnorm(y.reshape(b, L, SSD_GROUPS, SSD_INNER // SSD_GROUPS),
                norm_w.reshape(SSD_GROUPS, SSD_INNER // SSD_GROUPS))
    return y.reshape(b, L, SSD_INNER)


def fourier_mix(f):
    b, L, _ = f.shape
    fg = f.astype(jnp.float32).reshape(b, L, FFT_GROUPS, FFT_GROUP_DIM)
    y = jnp.fft.fft2(fg, axes=(1, 3), norm='ortho').real
    return y.reshape(b, L, FFT_WIDTH).astype(f.dtype)


def spatial_gating(uv, w_s, b_s):
    b, L, _ = uv.shape
    u, v = jnp.split(jax.nn.gelu(uv), 2, axis=-1)
    nc = L // GMLP_CHUNK
    vg = v.reshape(b, nc, GMLP_CHUNK, GMLP_GROUPS, GMLP_GROUP_DIM)
    s = jnp.einsum('gqs,bcsgd->bcqgd', w_s, vg) + jnp.transpose(b_s)[:, :, None]
    return u * s.reshape(b, L, GMLP_WIDTH)


def merge_branches(y_ssd, y_fft, y_gmlp, gate_logits, w_ssd_o, w_fft_o, w_gmlp_o, w_out):
    g = jax.nn.sigmoid(gate_logits)
    g0, g1, g2 = jnp.split(g, N_BRANCH, axis=-1)
    m = g0 * (y_ssd @ w_ssd_o) + g1 * (y_fft @ w_fft_o) + g2 * (y_gmlp @ w_gmlp_o)
    return m @ w_out


def token_mixer(hl, hc, w_in, conv_w, conv_b, a_log, dt_bias, d_skip, ssd_norm_w,
                w_s, b_s, w_ssd_o, w_fft_o, w_gmlp_o, w_out, need_ctx):
    xbc_l, dt_l, z_l, f_l, uv_l, gate_l = jnp.split(hl @ w_in, IN_SPLIT, axis=-1)
    if need_ctx:
        xbc_c, dt_c, z_c, f_c, uv_c, gate_c = jnp.split(hc @ w_in, IN_SPLIT, axis=-1)
    else:
        xbc_c, dt_c = jnp.split(hc @ w_in[:, :SSD_COLS], [SSD_XBC], axis=-1)
    xs_c, bm_c, cm_c, dtc = ssd_prep(xbc_c, dt_c, conv_w, conv_b, dt_bias)
    zero = jnp.zeros((hc.shape[0], SSD_GROUPS, SSD_HEADS // SSD_GROUPS, SSD_HEAD_DIM, SSD_STATE),
                     jnp.float32)
    y_c, s_f, s_b = ssd_bidir(xs_c, bm_c, cm_c, dtc, a_log, zero, zero, need_ctx)
    xs_l, bm_l, cm_l, dtl = ssd_prep(xbc_l, dt_l, conv_w, conv_b, dt_bias)
    y_l, _, _ = ssd_bidir(xs_l, bm_l, cm_l, dtl, a_log, s_f, s_b, True)
    out_l = merge_branches(ssd_output(y_l, xs_l, z_l, d_skip, ssd_norm_w), fourier_mix(f_l),
                           spatial_gating(uv_l, w_s, b_s), gate_l, w_ssd_o, w_fft_o, w_gmlp_o, w_out)
    if not need_ctx:
        return out_l, None
    out_c = merge_branches(ssd_output(y_c, xs_c, z_c, d_skip, ssd_norm_w), fourier_mix(f_c),
                           spatial_gating(uv_c, w_s, b_s), gate_c, w_ssd_o, w_fft_o, w_gmlp_o, w_out)
    return out_l, out_c


def conv_ffn(h, w_up, conv_w, conv_b, w_down, on_grid):
    u = h @ w_up
    b, L, C = u.shape
    if on_grid:
        rows = L // GRID_W
        u = dwconv(u.reshape(b * rows, GRID_W, C), conv_w, conv_b).reshape(b, L, C)
    else:
        u = dwconv(u, conv_w, conv_b)
    a, v = jnp.split(u, 2, axis=-1)
    return (jax.nn.silu(a) * v) @ w_down


def setup_inputs(seed: int = 0) -> dict:
    key = jax.random.key(seed)
    ks = jax.random.split(key, 32)
    D, H = D_MODEL, SSD_HEADS
    nrm = lambda k, shape, fan_in, s=1.0: s * jax.random.normal(k, shape, jnp.float32) * fan_in ** -0.5
    dt0 = jnp.exp(jax.random.uniform(ks[10], (DEPTH, 2, H), jnp.float32, math.log(1e-3), math.log(1e-1)))
    return {
        "x": jax.random.normal(ks[0], (BATCH, SEQ, D), jnp.float32),
        "c": jax.random.normal(ks[1], (BATCH, D), jnp.float32),
        "ctx": jax.random.normal(ks[2], (BATCH, CTX_LEN, D), jnp.float32),
        "c_ctx": jax.random.normal(ks[3], (D,), jnp.float32),
        "w_mod": nrm(ks[4], (DEPTH, D, 6 * D), D, 0.5),
        "b_mod": 0.02 * jax.random.normal(ks[5], (DEPTH, 6 * D), jnp.float32),
        "norm1_w": 1.0 + 0.02 * jax.random.normal(ks[6], (DEPTH, D), jnp.float32),
        "w_in": nrm(ks[7], (DEPTH, D, IN_COLS), D),
        "ssd_conv_w": nrm(ks[8], (DEPTH, SSD_CONV, SSD_XBC), SSD_CONV),
        "ssd_conv_b": 0.02 * jax.random.normal(ks[9], (DEPTH, SSD_XBC), jnp.float32),
        "ssd_a_log": jnp.log(jax.random.uniform(ks[11], (DEPTH, 2, H), jnp.float32, 1.0, 16.0)),
        "ssd_dt_bias": dt0 + jnp.log(-jnp.expm1(-dt0)),
        "ssd_d": 1.0 + 0.02 * jax.random.normal(ks[12], (DEPTH, H), jnp.float32),
        "ssd_norm_w": 1.0 + 0.02 * jax.random.normal(ks[13], (DEPTH, SSD_INNER), jnp.float32),
        "gmlp_w_s": nrm(ks[14], (DEPTH, GMLP_GROUPS, GMLP_CHUNK, GMLP_CHUNK), GMLP_CHUNK),
        "gmlp_b_s": 1.0 + 0.02 * jax.random.normal(ks[15], (DEPTH, GMLP_GROUPS, GMLP_CHUNK), jnp.float32),
        "w_ssd_o": nrm(ks[16], (DEPTH, SSD_INNER, D), SSD_INNER),
        "w_fft_o": nrm(ks[17], (DEPTH, FFT_WIDTH, D), FFT_WIDTH),
        "w_gmlp_o": nrm(ks[18], (DEPTH, GMLP_WIDTH, D), GMLP_WIDTH),
        "w_out": nrm(ks[19], (DEPTH, D, D), D),
        "norm2_w": 1.0 + 0.02 * jax.random.normal(ks[20], (DEPTH, D), jnp.float32),
        "ffn_w_up": nrm(ks[21], (DEPTH, D, 2 * D_FF), D),
        "ffn_conv_w": nrm(ks[22], (DEPTH, FFN_CONV, 2 * D_FF), FFN_CONV),
        "ffn_conv_b": 0.02 * jax.random.normal(ks[23], (DEPTH, 2 * D_FF), jnp.float32),
        "ffn_w_down": nrm(ks[24], (DEPTH, D_FF, D), D_FF),
        "final_norm_w": 1.0 + 0.02 * jax.random.normal(ks[25], (D,), jnp.float32),
    }


def reference(x, c, ctx, c_ctx, w_mod, b_mod, norm1_w, w_in, ssd_conv_w, ssd_conv_b, ssd_a_log,
              ssd_dt_bias, ssd_d, ssd_norm_w, gmlp_w_s, gmlp_b_s, w_ssd_o, w_fft_o, w_gmlp_o, w_out,
              norm2_w, ffn_w_up, ffn_conv_w, ffn_conv_b, ffn_w_down, final_norm_w):
    xl, xc = x, ctx
    for i in range(DEPTH):
        need_ctx = i < DEPTH - 1
        mod_l = (jax.nn.silu(c) @ w_mod[i] + b_mod[i])[:, None, :]
        mod_c = jax.nn.silu(c_ctx) @ w_mod[i] + b_mod[i]
        sh1, sc1, g1, sh2, sc2, g2 = jnp.split(mod_l, 6, axis=-1)
        csh1, csc1, cg1, csh2, csc2, cg2 = jnp.split(mod_c, 6, axis=-1)
        hl = modulate(rmsnorm(xl, norm1_w[i]), sh1, sc1)
        hc = modulate(rmsnorm(xc, norm1_w[i]), csh1, csc1)
        out_l, out_c = token_mixer(hl, hc, w_in[i], ssd_conv_w[i], ssd_conv_b[i], ssd_a_log[i],
                                   ssd_dt_bias[i], ssd_d[i], ssd_norm_w[i], gmlp_w_s[i], gmlp_b_s[i],
                                   w_ssd_o[i], w_fft_o[i], w_gmlp_o[i], w_out[i], need_ctx)
        xl = xl + g1 * out_l
        hl2 = modulate(rmsnorm(xl, norm2_w[i]), sh2, sc2)
        xl = xl + g2 * conv_ffn(hl2, ffn_w_up[i], ffn_conv_w[i], ffn_conv_b[i], ffn_w_down[i], True)
        if need_ctx:
            xc = xc + cg1 * out_c
            hc2 = modulate(rmsnorm(xc, norm2_w[i]), csh2, csc2)
            xc = xc + cg2 * conv_ffn(hc2, ffn_w_up[i], ffn_conv_w[i], ffn_conv_b[i], ffn_w_down[i], False)
    return rmsnorm(xl, final_norm_w)
```

```python
import numpy as np
import ml_dtypes
import concourse.bass as bass
import concourse.mybir as mybir
from concourse.bass_utils import run_bass_kernel_spmd
from contextlib import ExitStack

F32 = mybir.dt.float32
BF16 = mybir.dt.bfloat16
AF = mybir.ActivationFunctionType
ALU = mybir.AluOpType

D = 1024
SEQ = 2048
CTXL = 256
NT = 18
TOK = 2304
DFF = 2816
OFF_DT = 1536
OFF_Z = 1568
OFF_FFT = 2592
OFF_GMLP = 3104
OFF_GATE = 4128
IN_COLS = 7200
EPS = 1e-6
NCORES = 8
SAME_ENGINE_SYNC = True


class Buf:
    __slots__ = ("name", "w", "r")

    def __init__(self, name):
        self.name = name
        self.w = {}
        self.r = {}


class Sched:
    ND = 12

    def __init__(self, nc, es):
        self.nc = nc
        self.eng = {"pe": nc.tensor, "act": nc.scalar, "dve": nc.vector, "pool": nc.gpsimd, "sp": nc.sync}
        self.sem = {e: es.enter_context(nc.semaphore("s_" + e)) for e in ("pe", "act", "dve", "pool")}
        self.cnt = {e: 0 for e in self.sem}
        self.dsem = {q: [es.enter_context(nc.semaphore("d_%s_%d" % (q, i))) for i in range(self.ND)] for q in ("sp", "pool", "act")}
        self.dval = {q: [0] * self.ND for q in ("sp", "pool", "act")}
        self.dlast = {q: [None] * self.ND for q in ("sp", "pool", "act")}
        self.dnext = {"sp": 0, "pool": 0, "act": 0}
        self.waited = {e: {} for e in self.eng}
        self.nops = 0
        self.marks = []
        self.nsl = {e: 0 for e in self.sem}

    def mark(self, name):
        self.marks.append((name, dict(self.nsl)))

    def ensure(self, e, tok):
        if tok is None:
            return
        key, val, sem = tok
        if key == e and (e == "pe" or not SAME_ENGINE_SYNC):
            return
        w = self.waited[e]
        if w.get(key, 0) >= val:
            return
        self.eng[e].wait_ge(sem, val)
        w[key] = val

    def _deps(self, e, reads, writes, append=False):
        for b in reads:
            for t in b.w.values():
                self.ensure(e, t)
        for b in writes:
            if not append:
                for t in b.w.values():
                    self.ensure(e, t)
            for t in b.r.values():
                self.ensure(e, t)

    def _mark(self, tok, reads, writes, append=False):
        k = tok[0]
        for b in reads:
            b.r[k] = tok
        for b in writes:
            if append:
                b.w[k] = tok
            else:
                b.w = {k: tok}
            b.r = {}

    def op(self, e, fn, reads=(), writes=(), nsl=1):
        self._deps(e, reads, writes)
        ins = fn(self.eng[e])
        self.cnt[e] += 1
        self.nsl[e] += nsl
        tok = (e, self.cnt[e], self.sem[e])
        ins.then_inc(self.sem[e], 1)
        self._mark(tok, reads, writes)
        self.nops += 1
        return tok

    def dma(self, q, out, in_, reads=(), writes=(), append=False, **kw):
        self._deps(q, reads, writes, append)
        i = self.dnext[q]
        self.dnext[q] = (i + 1) % self.ND
        self.ensure(q, self.dlast[q][i])
        ins = self.eng[q].dma_start(out=out, in_=in_, **kw)
        self.dval[q][i] += 16
        sem = self.dsem[q][i]
        ins.then_inc(sem, 16)
        tok = (("d", q, i), self.dval[q][i], sem)
        self.dlast[q][i] = tok
        self._mark(tok, reads, writes, append)
        self.nops += 1
        return tok

    def barrier(self, engines=("pe", "act", "dve", "pool", "sp")):
        toks = [(e, self.cnt[e], self.sem[e]) for e in self.sem if self.cnt[e] > 0]
        for q in self.dlast:
            toks += [t for t in self.dlast[q] if t is not None]
        for e in engines:
            for t in toks:
                self.ensure(e, t)


class Region:
    def __init__(self, arena, base, size):
        self.arena = arena
        self.base = base
        self.size = size
        self.off = 0

    def reset(self):
        self.off = 0

    def alloc(self, name, shape, dtype, parts=128):
        esz = 4 if dtype == F32 else 2
        n = 1
        for s in shape:
            n *= s
        nb = (n * esz + 63) // 64 * 64
        assert self.off + nb <= self.size, (name, self.off, nb, self.size)
        o = (self.base + self.off) // 2
        self.off += nb
        ap = self.arena[0:parts, o:o + n * esz // 2]
        if dtype == F32:
            ap = ap.bitcast(F32)
        if len(shape) == 2:
            ap = ap.rearrange("p (a b) -> p a b", a=shape[0])
        elif len(shape) == 3:
            ap = ap.rearrange("p (a b c) -> p a b c", a=shape[0], b=shape[1])
        return ap, Buf(name)


def build(NB, NL=2):
    NB1 = NB + 1
    nc = bass.Bass("TRN2", target_bir_lowering=False)
    es = ExitStack()

    def din(name, shape, dt=F32):
        return nc.dram_tensor(name, list(shape), dt, kind="ExternalInput").ap()

    def dscr(name, shape, dt):
        return nc.dram_tensor(name, list(shape), dt, kind="Internal").ap()

    x_in = din("x", [NB, SEQ, D])
    ctx_in = din("ctx", [NB, CTXL, D])
    ccT_in = din("ccT", [128, 8, NB1])
    w_mod = din("w_mod", [NL, D, 6 * D])
    bmod_row = din("b_mod", [NL, 6 * D])
    bmodT_in = din("bmodT", [128, NL, 48])
    n1T_in = din("n1T", [128, NL, 8])
    n2T_in = din("n2T", [128, NL, 8])
    fn_in = din("final_norm_w", [D])
    w_in = din("w_in", [NL, D, IN_COLS])
    cwT_in = din("cwT", [128, NL, 12, 5])
    cbT_in = din("cbT", [128, NL, 12])
    alog_in = din("ssd_a_log", [NL, 32])
    dtb_in = din("ssd_dt_bias", [NL, 32])
    dsk_in = din("ssd_d", [NL, 16])
    snT_in = din("snT", [128, NL, 8])
    wsT_in = din("wsT", [NL, 128, 4, 128])
    bs_in = din("gmlp_b_s", [NL, 512])
    w_sso = din("w_ssd_o", [NL, D, D])
    w_ffo = din("w_fft_o", [NL, 512, D])
    w_gmo = din("w_gmlp_o", [NL, 512, D])
    w_out = din("w_out", [NL, D, D])
    w_up = din("ffn_w_up", [NL, D, 2 * DFF])
    fcwT_in = din("fcwT", [128, NL, 44, 3])
    fcbT_in = din("fcbT", [128, NL, 44])
    w_dn = din("ffn_w_down", [NL, DFF, D])
    ident_in = din("ident", [128, 128], BF16)
    trif_in = din("trif", [128, 128])
    trib_in = din("trib", [128, 128])
    ones_in = din("ones", [128, 128])
    sel3_in = din("sel3", [96, 16, 128], BF16)
    identF_in = din("identF", [128, 128])
    negm_in = din("negm", [128, 2, 128], BF16)
    cs_in = din("cs", [128, 256], BF16)
    cst_in = din("cst", [4, 8, 128, 2 * 2 * 512], BF16)
    cst256_in = din("cst256", [1, 1, 128, 2 * 2 * 256], BF16)
    out_d = nc.dram_tensor("out", [NB, SEQ, D], F32, kind="ExternalOutput").ap()

    wq_x = dscr("wq_x", [NL, 3, 128, 8 * 512], BF16)
    wq_dt = dscr("wq_dt", [NL, 1, 128, 8 * 32], BF16)
    wq_z = dscr("wq_z", [NL, 1, 128, 8 * 1024], BF16)
    wq_f = dscr("wq_f", [NL, 1, 128, 8 * 512], BF16)
    wq_uv = dscr("wq_uv", [NL, 1, 128, 8 * 1024], BF16)
    wq_g = dscr("wq_g", [NL, 4, 128, 24 * 256], BF16)
    wq_mo = dscr("wq_mo", [NL, 4, 128, 16 * 256], BF16)
    wq_out = dscr("wq_out", [NL, 2, 128, 8 * 512], BF16)
    wq_ua = dscr("wq_ua", [NL, 6, 128, 8 * 512], BF16)
    wq_uvv = dscr("wq_uvv", [NL, 6, 128, 8 * 512], BF16)
    wq_dn = dscr("wq_dn", [NL, 4, 128, 22 * 256], BF16)
    modrow_d = dscr("modrow_d", [NL, NB1, 6 * D], F32)
    xres_d = dscr("xres_d", [NB, SEQ, D], F32)
    ys_d = dscr("ys_d", [NB, NT, 128, 1024], BF16)
    yf_d = dscr("yf_d", [NB, NT, 128, 512], BF16)
    yg_d = dscr("yg_d", [NB, NT, 128, 512], BF16)
    sst_d = dscr("sst_d", [NB, 2, NT, 128, D], BF16)

    S = Sched(nc, es)
    TOTB = 212000
    PERMSZ = 24064
    arena = es.enter_context(nc.sbuf_tensor("arena", [128, TOTB // 2], BF16))
    PERM = Region(arena, 0, PERMSZ)
    HT = Region(arena, PERMSZ, 36864)
    WPB = PERMSZ + 36864
    WP = [Region(arena, WPB + i * 16384, 16384) for i in range(2)]
    R3 = Region(arena, WPB + 32768, 73728)
    R4 = Region(arena, WPB + 32768 + 73728, TOTB - (WPB + 32768 + 73728))
    wp_buf = [Buf("wp0"), Buf("wp1")]
    wp_next = [0]

    P = [es.enter_context(nc.psum_tensor("P%d" % i, [128, 1024], F32)) for i in range(4)]
    pbuf = [[Buf("p%d_%d" % (i, h)) for h in range(2)] for i in range(4)]
    bank_rr = [0]

    def next_bank():
        k = bank_rr[0]
        bank_rr[0] = (k + 1) % 8
        i, h = k // 2, k % 2
        return P[i][:, h * 512:(h + 1) * 512], pbuf[i][h]

    dbl_rr = [0]

    def next_dbl():
        i = dbl_rr[0]
        dbl_rr[0] = (i + 1) % 4
        return P[i][:, :], pbuf[i]

    def mm(out, lhsT, rhs, start, stop, reads, writes):
        S.op("pe", lambda e: e.matmul(out, lhsT=lhsT, rhs=rhs, start=start, stop=stop), reads, writes)

    def acc(out, obufs, terms):
        n = len(terms)
        for i, (l, r, rb) in enumerate(terms):
            mm(out, l, r, i == 0, i == n - 1, rb, obufs)

    def tr(out, in_, reads, writes):
        S.op("pe", lambda e: e.transpose(out, in_, ident), list(reads) + [b_const], writes)

    def wload(src_ap, shape, name="w", l=0):
        i = wp_next[0]
        wp_next[0] = 1 - i
        WP[i].reset()
        ap, _ = WP[i].alloc(name, shape, BF16)
        S.dma("sp", ap, src_ap, [b_wqd[("in", l)]], [wp_buf[i]])
        return ap, wp_buf[i]

    def kview(w2d):
        return w2d.rearrange("(kc p) n -> p kc n", p=128)

    b_const = Buf("const")
    b_wq = Buf("wq")
    ident, _ = PERM.alloc("ident", [128], BF16)
    trif, _ = PERM.alloc("trif", [128], F32)
    trib, _ = PERM.alloc("trib", [128], F32)
    ones, _ = PERM.alloc("ones", [128], F32)
    sel3, _ = PERM.alloc("sel3", [16, 128], BF16, parts=96)
    identF, _ = PERM.alloc("identF", [128], F32)
    negm, _ = PERM.alloc("negm", [2, 128], BF16)
    cs, _ = PERM.alloc("cs", [256], BF16)
    wsT, _ = PERM.alloc("wsT", [NL, 4, 128], BF16)
    bs_bc, _ = PERM.alloc("bs_bc", [NL, 512], F32)
    dsk_bc, _ = PERM.alloc("dsk_bc", [NL, 16], F32)
    dtb_bc, _ = PERM.alloc("dtb_bc", [NL, 32], F32)
    a_bc, _ = PERM.alloc("a_bc", [NL, 32], F32)
    fw_bc, _ = PERM.alloc("fw_bc", [D], F32)
    modT, b_modT = PERM.alloc("modT", [NL, 48, NB1], F32)
    s1T, _ = PERM.alloc("s1T", [NL, 8, NB1], F32)
    s2T, _ = PERM.alloc("s2T", [NL, 8, NB1], F32)
    n1T, _ = PERM.alloc("n1T", [NL, 8], F32)
    n2T, _ = PERM.alloc("n2T", [NL, 8], F32)
    bmodT, _ = PERM.alloc("bmodT", [NL, 48], F32)
    cwT, _ = PERM.alloc("cwT", [NL, 12, 5], F32)
    cbT, _ = PERM.alloc("cbT", [NL, 12], F32)
    fcwT, _ = PERM.alloc("fcwT", [NL, 44, 3], F32)
    fcbT, _ = PERM.alloc("fcbT", [NL, 44], F32)
    snT, _ = PERM.alloc("snT", [NL, 8], F32)
    sc, _ = PERM.alloc("sc", [8, NB1], F32)
    ss, b_ss = PERM.alloc("ss", [4], F32)
    ss2, b_ss2 = PERM.alloc("ss2", [4], F32)
    rstd, b_rstd = PERM.alloc("rstd", [4], F32)
    eps_c, _ = PERM.alloc("eps_c", [1], F32)
    hT, _ = HT.alloc("hT", [8, TOK], BF16)
    b_hT = [Buf("hT%d" % t) for t in range(NT)]

    def ld(dst, src, **kw):
        S.dma("sp", dst, src, [], [b_const], append=True, **kw)

    ld(ident, ident_in)
    ld(trif, trif_in)
    ld(trib, trib_in)
    ld(ones, ones_in)
    ld(sel3, sel3_in)
    ld(identF, identF_in)
    ld(negm, negm_in)
    ld(cs, cs_in)
    ld(n1T, n1T_in)
    ld(n2T, n2T_in)
    ld(bmodT, bmodT_in)
    ld(cwT, cwT_in)
    ld(cbT, cbT_in)
    ld(fcwT, fcwT_in)
    ld(fcbT, fcbT_in)
    ld(snT, snT_in)
    ld(sc, ccT_in)
    for l in range(NL):
        ld(bs_bc[:, l, :], bs_in[l].partition_broadcast(128))
        ld(dsk_bc[:, l, :], dsk_in[l].partition_broadcast(128))
        ld(dtb_bc[:, l, :], dtb_in[l].partition_broadcast(128))
        ld(a_bc[:, l, :], alog_in[l].partition_broadcast(128))
        S.dma("pool", wsT[:, l], wsT_in[l], [], [b_const], append=True)
    ld(fw_bc, fn_in.partition_broadcast(128))
    S.op("pool", lambda e: e.memset(eps_c, EPS), [], [b_const])
    S.op("act", lambda e: e.activation(out=a_bc, in_=a_bc, func=AF.Exp), [b_const], [b_const])
    S.op("dve", lambda e: e.tensor_scalar(out=a_bc, in0=a_bc, scalar1=-1.0, scalar2=None, op0=ALU.mult), [b_const], [b_const])
    S.op("act", lambda e: e.activation(out=sc, in_=sc, func=AF.Silu), [b_const], [b_const])

    b_wqd = {}

    def conv(key, l, dst2d, kc, w, src3d, off=0):
        bq = b_wqd.setdefault((key, l), Buf("wq_%s%d" % (key, l)))
        S.dma("pool", dst2d[:, off:off + kc * w].rearrange("p (k w) -> p k w", k=kc), src3d, [], [bq], append=True, max_dma_last_dim=4096)

    for l in range(NL):
        wi = kview(w_in[l])
        for cb in range(3):
            conv("in", l, wq_x[l, cb], 8, 512, wi[:, :, cb * 512:(cb + 1) * 512])
        conv("in", l, wq_dt[l, 0], 8, 32, wi[:, :, OFF_DT:OFF_DT + 32])
        conv("in", l, wq_z[l, 0], 8, 1024, wi[:, :, OFF_Z:OFF_Z + 1024])
        conv("in", l, wq_f[l, 0], 8, 512, wi[:, :, OFF_FFT:OFF_FFT + 512])
        conv("in", l, wq_uv[l, 0], 8, 1024, wi[:, :, OFF_GMLP:OFF_GMLP + 1024])
        for jb in range(4):
            for br in range(3):
                conv("in", l, wq_g[l, jb], 8, 256, wi[:, :, OFF_GATE + br * D + jb * 256:OFF_GATE + br * D + (jb + 1) * 256], off=br * 2048)
            conv("mo", l, wq_mo[l, jb], 8, 256, kview(w_sso[l])[:, :, jb * 256:(jb + 1) * 256], off=0)
            conv("mo", l, wq_mo[l, jb], 4, 256, kview(w_ffo[l])[:, :, jb * 256:(jb + 1) * 256], off=2048)
            conv("mo", l, wq_mo[l, jb], 4, 256, kview(w_gmo[l])[:, :, jb * 256:(jb + 1) * 256], off=3072)
        for hf in range(2):
            conv("out", l, wq_out[l, hf], 8, 512, kview(w_out[l])[:, :, hf * 512:(hf + 1) * 512])
        for pb6 in range(6):
            ncol = (4 if pb6 < 5 else 2) * 128
            conv("up", l, wq_ua[l, pb6], 8, ncol, kview(w_up[l])[:, :, pb6 * 512:pb6 * 512 + ncol])
            conv("up", l, wq_uvv[l, pb6], 8, ncol, kview(w_up[l])[:, :, DFF + pb6 * 512:DFF + pb6 * 512 + ncol])
        for qt in range(4):
            conv("dn", l, wq_dn[l, qt], 22, 256, kview(w_dn[l])[:, :, qt * 256:(qt + 1) * 256])

    R3.reset()
    wm = [R3.alloc("wm%d" % i, [8, 512], F32) for i in range(2)]
    mrows = [R3.alloc("mrow%d" % i, [512], F32, parts=NB1) for i in range(2)]
    bmrs = [R3.alloc("bmr%d" % i, [512], F32, parts=NB1) for i in range(2)]
    b_modrow = Buf("modrow")
    for l in range(NL):
        for blk in range(12):
            wt, wb = wm[blk % 2]
            S.dma("sp", wt, kview(w_mod[l])[:, :, blk * 512:(blk + 1) * 512], [], [wb])
            bmr, b_bmr = bmrs[blk % 2]
            S.dma("sp", bmr, bmod_row[l, blk * 512:(blk + 1) * 512].partition_broadcast(NB1), [], [b_bmr])
            mrow, b_mrow = mrows[blk % 2]
            pa, pb_ = next_bank()
            acc(pa[0:NB1, :], [pb_], [(sc[:, kc, :], wt[:, kc, :], [wb, b_const]) for kc in range(8)])
            S.op("dve", lambda e: e.tensor_tensor(out=mrow, in0=pa[0:NB1, :], in1=bmr, op=ALU.add),
                 [pb_, b_bmr], [b_mrow])
            S.dma("pool", modrow_d[l, :, blk * 512:(blk + 1) * 512], mrow, [b_mrow], [b_modrow], append=True)
            pa, pb_ = next_bank()
            for j in range(4):
                mm(pa[:, j * 8:j * 8 + NB1], mrow[0:NB1, j * 128:(j + 1) * 128], identF[0:NB1, 0:NB1], True, True, [b_mrow, b_const], [pb_])
            S.op("dve", lambda e: e.tensor_copy(out=modT[:, l, blk * 4:(blk + 1) * 4, :], in_=pa[:, 0:32].rearrange("p (j c) -> p j c", j=4)[:, :, 0:NB1]),
                 [pb_], [b_modT])
    for l in range(NL):
        S.op("dve", lambda e: e.scalar_tensor_tensor(out=s1T[:, l], in0=modT[:, l, 8:16, :], scalar=1.0,
                                                      in1=n1T[:, l, :].unsqueeze(2).broadcast_to([128, 8, NB1]), op0=ALU.add, op1=ALU.mult),
             [b_modT, b_const], [b_modT])
        S.op("dve", lambda e: e.scalar_tensor_tensor(out=s2T[:, l], in0=modT[:, l, 32:40, :], scalar=1.0,
                                                      in1=n2T[:, l, :].unsqueeze(2).broadcast_to([128, 8, NB1]), op0=ALU.add, op1=ALU.mult),
             [b_modT, b_const], [b_modT])
    S.barrier()

    def tcol(t):
        return t * 128

    def acol(t):
        return t * 128 + (4 if t >= 2 else 0)

    GROUPS = [(0, 2)] + [(2 + 4 * i, 4) for i in range(4)]

    def rstd_from_ss(k, n):
        S.op("act", lambda e: e.activation(out=ss2[:, k:k + 1], in_=ss[:, k:k + 1], func=AF.Ln, scale=1.0 / n, bias=eps_c[:, 0:1]), [b_ss, b_const], [b_ss2])
        S.op("act", lambda e: e.activation(out=rstd[:, k:k + 1], in_=ss2[:, k:k + 1], func=AF.Exp, scale=-0.5), [b_ss2], [b_rstd])

    def norm_front(xt, xb, xn, b_xn):
        S.op("dve", lambda e: e.scalar_tensor_tensor(out=xn, in0=xt, scalar=1.0, in1=xt, op0=ALU.mult, op1=ALU.mult, accum_out=ss[:, 0:1]), xb, [b_xn, b_ss], nsl=2)
        rstd_from_ss(0, D)
        S.op("dve", lambda e: e.tensor_scalar(out=xn, in0=xt, scalar1=rstd[:, 0:1], scalar2=None, op0=ALU.mult), list(xb) + [b_rstd], [b_xn])

    def norm_back(xn, b_xn, l, bi, which, dstT, dst_bufs, dcol):
        sT = s1T if which == 1 else s2T
        boff = 0 if which == 1 else 24
        pa, pbs = next_dbl()
        pT = pa[:, 0:512].bitcast(BF16)
        for dc in range(8):
            tr(pT[:, dc * 128:(dc + 1) * 128], xn[:, dc * 128:(dc + 1) * 128], [b_xn], [pbs[0]])
        for dc in range(8):
            eng = "act" if dc % 2 == 0 else "dve"
            o = dstT[:, dc, dcol:dcol + 128]
            i_ = pT[:, dc * 128:(dc + 1) * 128]
            sA = sT[:, l, dc, bi:bi + 1]
            bA = modT[:, l, boff + dc, bi:bi + 1]
            if eng == "act":
                S.op("act", lambda e: e.activation(out=o, in_=i_, func=AF.Identity, scale=sA, bias=bA), [pbs[0], b_modT], dst_bufs)
            else:
                S.op("dve", lambda e: e.tensor_scalar(out=o, in0=i_, scalar1=sA, scalar2=bA, op0=ALU.mult, op1=ALU.add), [pbs[0], b_modT], dst_bufs)

    def norm_to_hT(xt, xb, l, bi, which, t, junk, b_junk, xn, b_xn, dstT=None, dst_bufs=None, dcol=None):
        norm_front(xt, xb, xn, b_xn)
        norm_back(xn, b_xn, l, bi, which, dstT, dst_bufs, dcol)

    def xrow(b, t):
        if t < 2:
            return ctx_in[b, t * 128:(t + 1) * 128, :]
        return x_in[b, (t - 2) * 128:(t - 1) * 128, :]

    def phase1(b):
        S.mark("phase1 b%d" % b)
        R4.reset()
        xl = [R4.alloc("xl%d" % i, [D], F32) for i in range(3)]
        xns = [R4.alloc("xn%d" % i, [D], BF16) for i in range(2)]
        for t in range(NT + 1):
            if t < NT:
                xt, xb = xl[t % 3]
                S.dma("sp", xt, xrow(b, t), [], [xb])
                norm_front(xt, [xb], *xns[t % 2])
            if t >= 1:
                tp_ = t - 1
                bi = NB if tp_ < 2 else b
                norm_back(xns[tp_ % 2][0], xns[tp_ % 2][1], 0, bi, 1, hT, [b_hT[tp_]], tcol(tp_))
        S.barrier()

    def phaseI(b, l):
        tiles_y = list(range(NT)) if l == 0 else list(range(2, NT))
        S.mark("P2 b%d l%d" % (b, l))
        R3.reset()
        R4.reset()
        xs_tok, b_xs = R3.alloc("xs_tok", [NT, D], BF16)
        B_tok, b_Bt = R3.alloc("B_tok", [NT, 256], BF16)
        BT, b_BT = R3.alloc("BT", [2, 2312], BF16)
        CT, b_CT = R3.alloc("CT", [2, 2312], BF16)
        raws = [R4.alloc("raw%d" % i, [2312], BF16) for i in range(2)]
        caccs = [R4.alloc("cacc%d" % i, [2308], F32) for i in range(2)]
        actc, b_actc = R4.alloc("actc", [2312], BF16)
        for i in range(2):
            S.op("pool", lambda e: e.memset(raws[i][0], 0.0), [], [raws[i][1]])
        def p2_trans(c, dst, dbuf):
            for t0 in range(0, NT, 6):
                pa, pbs = next_dbl()
                pT = pa.bitcast(BF16)
                for i in range(6):
                    t = t0 + i
                    tr(pT[:, i * 128:(i + 1) * 128], dst[:, acol(t):acol(t) + 128], [dbuf], [pbs[0]])
                if c < 8:
                    o = xs_tok[:, t0:t0 + 6, c * 128:(c + 1) * 128]
                    ob = b_xs
                else:
                    o = B_tok[:, t0:t0 + 6, (c - 8) * 128:(c - 7) * 128]
                    ob = b_Bt
                S.op("act", lambda e: e.activation(out=o, in_=pT[:, 0:768].rearrange("p (a b) -> p a b", a=6), func=AF.Identity), [pbs[0]], [ob])

        wts = {}

        def p2_A(c):
            cb, j = c // 4, c % 4
            if j == 0:
                wts[cb] = wload(wq_x[l, cb].rearrange("p (k w) -> p k w", k=8), [8, 512], "w", l=l)
            wt, wb = wts[cb]
            raw, b_raw = raws[c % 2]
            cacc, b_cacc = caccs[c % 2]
            for (t0, ntl) in GROUPS:
                n = ntl * 128
                c0 = tcol(t0)
                pa, pb_ = next_bank()
                acc(pa[:, 0:n], [pb_], [(wt[:, kc, j * 128:(j + 1) * 128], hT[:, kc, c0:c0 + n], [wb] + b_hT[t0:t0 + ntl]) for kc in range(8)])
                r0 = c0 + 2 if t0 == 0 else c0 + 6
                S.op("act", lambda e: e.activation(out=raw[:, r0:r0 + n], in_=pa[:, 0:n], func=AF.Identity), [pb_], [b_raw])
            S.op("act", lambda e: e.activation(out=cacc, in_=raw[:, 0:2308], func=AF.Identity, scale=cwT[:, l, c, 0:1]),
                 [b_raw, b_const], [b_cacc])

        def p2_B(c):
            raw, b_raw = raws[c % 2]
            cacc, b_cacc = caccs[c % 2]
            for k in range(1, 5):
                S.op("dve", lambda e: e.scalar_tensor_tensor(out=cacc, in0=raw[:, k:k + 2308], scalar=cwT[:, l, c, k:k + 1], in1=cacc,
                                                              op0=ALU.mult, op1=ALU.add), [b_raw, b_const, b_cacc], [b_cacc])

        def p2_C(c):
            cacc, b_cacc = caccs[c % 2]
            if c < 8:
                dst, dbuf = actc[:, 0:2308], b_actc
            elif c < 10:
                dst, dbuf = BT[:, c - 8, 0:2308], b_BT
            else:
                dst, dbuf = CT[:, c - 10, 0:2308], b_CT
            S.op("act", lambda e: e.activation(out=dst, in_=cacc, func=AF.Silu, bias=cbT[:, l, c:c + 1]), [b_cacc, b_const], [dbuf])
            return (c, dst, dbuf) if c < 10 else None

        p2_A(0)
        p2_B(0)
        for c in range(12):
            if c + 1 < 12:
                p2_A(c + 1)
            pend_tr = p2_C(c)
            if c + 1 < 12:
                p2_B(c + 1)
            if pend_tr is not None:
                p2_trans(*pend_tr)
        S.barrier()
        S.mark("P3 b%d l%d" % (b, l))
        R4.reset()
        la_full, b_la = R4.alloc("la", [NT + 1, 32], F32)
        bE_full, b_bE = R4.alloc("biasE", [NT + 1, 32], F32)
        la = la_full[:, 0:NT, :]
        biasE = bE_full[:, 0:NT, :]
        S.op("pool", lambda e: e.memset(la_full[:, NT, :], 0.0), [], [b_la])
        S.op("pool", lambda e: e.memset(bE_full[:, NT, :], 0.0), [], [b_bE])
        dq, b_dq = R4.alloc("dq", [NT, 32], F32)
        r4_p5 = R4.off
        dt_, b_dt = R4.alloc("dt", [NT, 32], F32)
        cum, b_cum = R4.alloc("cum", [NT, 32], F32)
        tot, b_tot = R4.alloc("tot", [NT, 32], F32)
        ww, b_ww = R4.alloc("ww", [NT, 32], F32)
        wdt_ap, wdt_b = wload(wq_dt[l, 0].rearrange("p (k w) -> p k w", k=8), [8, 32], "wdt", l=l)
        pa, pbs = next_dbl()
        for t in range(NT):
            acc(pa[:, t * 32:(t + 1) * 32], [pbs[t // 16]], [(hT[:, kc, tcol(t):tcol(t) + 128], wdt_ap[:, kc, :], [wdt_b, b_hT[t]]) for kc in range(8)])
        S.op("dve", lambda e: e.tensor_tensor(out=dt_, in0=pa[:, 0:NT * 32].rearrange("p (a b) -> p a b", a=NT),
                                               in1=dtb_bc[:, l, :].unsqueeze(1).broadcast_to([128, NT, 32]), op=ALU.add), [pbs[0], pbs[1], b_const], [b_dt])
        S.op("act", lambda e: e.activation(out=dt_, in_=dt_, func=AF.Exp), [b_dt], [b_dt])
        S.op("act", lambda e: e.activation(out=dt_, in_=dt_, func=AF.Ln, bias=1.0), [b_dt], [b_dt])
        S.op("dve", lambda e: e.tensor_tensor(out=la, in0=dt_, in1=a_bc[:, l, :].unsqueeze(1).broadcast_to([128, NT, 32]), op=ALU.mult), [b_dt, b_const], [b_la])
        S.op("act", lambda e: e.activation(out=biasE, in_=dt_, func=AF.Ln), [b_dt], [b_bE])
        pa, pbs = next_dbl()
        pa2, pbs2 = next_dbl()
        for t in range(NT):
            mm(pa[:, t * 32:t * 32 + 16], trif, la[:, t, 0:16], True, True, [b_la, b_const], [pbs[t // 16]])
            mm(pa[:, t * 32 + 16:t * 32 + 32], trib, la[:, t, 16:32], True, True, [b_la, b_const], [pbs[t // 16]])
            mm(pa2[:, t * 32:(t + 1) * 32], ones, la[:, t, :], True, True, [b_la, b_const], [pbs2[t // 16]])
        S.op("act", lambda e: e.activation(out=cum, in_=pa[:, 0:NT * 32].rearrange("p (a b) -> p a b", a=NT), func=AF.Identity), [pbs[0], pbs[1]], [b_cum])
        S.op("act", lambda e: e.activation(out=tot, in_=pa2[:, 0:NT * 32].rearrange("p (a b) -> p a b", a=NT), func=AF.Identity), [pbs2[0], pbs2[1]], [b_tot])
        S.op("dve", lambda e: e.tensor_tensor(out=biasE, in0=biasE, in1=cum, op=ALU.subtract), [b_bE, b_cum], [b_bE])
        S.op("act", lambda e: e.activation(out=dq, in_=cum, func=AF.Exp), [b_cum], [b_dq])
        S.op("dve", lambda e: e.tensor_tensor(out=ww, in0=tot, in1=cum, op=ALU.subtract), [b_tot, b_cum], [b_ww])
        S.op("act", lambda e: e.activation(out=ww, in_=ww, func=AF.Exp), [b_ww], [b_ww])
        S.op("dve", lambda e: e.tensor_tensor(out=ww, in0=ww, in1=dt_, op=ALU.mult), [b_ww, b_dt], [b_ww])
        S.op("act", lambda e: e.activation(out=tot, in_=tot, func=AF.Exp), [b_tot], [b_tot])
        S.mark("P4 b%d l%d" % (b, l))
        Sts = [R4.alloc("St%d" % i, [D], F32) for i in range(2)]
        tmpS, b_tmpS = R4.alloc("tmpS", [D], F32)
        xw, b_xw = R4.alloc("xw", [D], BF16)
        sst = [R4.alloc("sst%d" % i, [D], BF16) for i in range(2)]
        b_sstd = [[Buf("sstd%d_%d" % (d, t)) for t in range(NT)] for d in range(2)]
        groups = GROUPS if l == 0 else GROUPS[1:]
        uT, b_uT = R4.alloc("uT", [4, 512], BF16)
        vs = [R4.alloc("vs%d" % i, [512], BF16) for i in range(1)] * 2
        tmpg, b_tmpg = R4.alloc("tmpg", [512], F32)
        ygo = [R4.alloc("ygo%d" % i, [4, 128], BF16) for i in range(2)]
        wuv, wuvb = wload(wq_uv[l, 0].rearrange("p (k w) -> p k w", k=8), [8, 1024], "wuv", l=l)

        def gen_P4():
            k_sst = 0
            for d in range(2):
                order = list(range(NT)) if d == 0 else [1, 0] + list(range(NT - 1, 1, -1))
                cur = 0
                S.op("pool", lambda e: e.memset(Sts[0][0], 0.0), [], [Sts[0][1]])
                for t in order:
                    St, b_St = Sts[cur]
                    if t in tiles_y:
                        sa, sb = sst[k_sst % 2]
                        k_sst += 1
                        S.op("act", lambda e: e.activation(out=sa, in_=St, func=AF.Identity), [b_St], [sb])
                        S.dma("sp", sst_d[b, d, t], sa, [sb], [b_sstd[d][t]])
                    if t == order[-1]:
                        break
                    wv = ww[:, t, d * 16:(d + 1) * 16].unsqueeze(2).broadcast_to([128, 16, 64])
                    S.op("pool", lambda e: e.tensor_tensor(out=xw.rearrange("p (h c) -> p h c", h=16),
                                                            in0=xs_tok[:, t, :].rearrange("p (h c) -> p h c", h=16), in1=wv, op=ALU.mult),
                         [b_xs, b_ww], [b_xw])
                    pa, pbs = next_dbl()
                    for g in range(2):
                        mm(pa[:, g * 512:(g + 1) * 512], B_tok[:, t, g * 128:(g + 1) * 128], xw[:, g * 512:(g + 1) * 512], True, True, [b_Bt, b_xw], [pbs[g]])
                    dav = tot[:, t, d * 16:(d + 1) * 16].unsqueeze(2).broadcast_to([128, 16, 64])
                    S.op("dve", lambda e: e.tensor_tensor(out=tmpS.rearrange("p (h c) -> p h c", h=16), in0=St.rearrange("p (h c) -> p h c", h=16),
                                                           in1=dav, op=ALU.mult), [b_St, b_tot], [b_tmpS])
                    Sn, b_Sn = Sts[1 - cur]
                    S.op("dve", lambda e: e.tensor_tensor(out=Sn, in0=pa, in1=tmpS, op=ALU.add), [pbs[0], pbs[1], b_tmpS], [b_Sn])
                    cur = 1 - cur
                    yield

        def gen_P7():
            k_v = 0
            for (t0, ntl) in groups:
                n = ntl * 128
                c0 = tcol(t0)
                for g in range(4):
                    pa, pb_ = next_bank()
                    acc(pa[:, 0:n], [pb_], [(wuv[:, kc, g * 128:(g + 1) * 128], hT[:, kc, c0:c0 + n], [wuvb] + b_hT[t0:t0 + ntl]) for kc in range(8)])
                    S.op("act", lambda e: e.activation(out=uT[:, g, 0:n], in_=pa[:, 0:n], func=AF.Gelu_apprx_tanh), [pb_], [b_uT])
                    if g % 2 == 1:
                        yield
                for i in range(ntl):
                    t = t0 + i
                    pa, pb_ = next_bank()
                    acc(pa, [pb_], [(hT[:, kc, tcol(t):tcol(t) + 128], wuv[:, kc, 512:1024], [wuvb, b_hT[t]]) for kc in range(8)])
                    va, vb = vs[k_v % 2]
                    S.op("act", lambda e: e.activation(out=va, in_=pa, func=AF.Gelu_apprx_tanh), [pb_], [vb])
                    pa2, pb2 = next_bank()
                    for g in range(4):
                        mm(pa2[:, g * 128:(g + 1) * 128], va[:, g * 128:(g + 1) * 128], wsT[:, l, g, :], True, True, [vb, b_const], [pb2])
                    S.op("dve", lambda e: e.tensor_tensor(out=tmpg, in0=pa2, in1=bs_bc[:, l, :], op=ALU.add), [pb2, b_const], [b_tmpg])
                    yo, yob = ygo[k_v % 2]
                    k_v += 1
                    S.op("dve", lambda e: e.tensor_tensor(out=yo, in0=tmpg.rearrange("p (g q) -> p g q", g=4), in1=uT[:, :, i * 128:(i + 1) * 128], op=ALU.mult),
                         [b_tmpg, b_uT], [yob])
                    S.dma("sp", yg_d[b, t].rearrange("p (c n) -> p c n", c=4), yo, [yob], [b_ygd], append=True)
                    yield

        S.mark("P7 b%d l%d" % (b, l))
        g4, g7 = gen_P4(), gen_P7()
        d4 = d7 = False
        while not (d4 and d7):
            if not d4:
                try:
                    next(g4)
                except StopIteration:
                    d4 = True
            if not d7:
                try:
                    next(g7)
                except StopIteration:
                    d7 = True
        S.barrier()
        S.mark("P5 b%d l%d" % (b, l))
        R4.off = r4_p5
        wz, wzb = wload(wq_z[l, 0].rearrange("p (k w) -> p k w", k=8), [8, D], "wz", l=l)
        Sp = [R3.alloc("Sp%d" % d, [D], BF16) for d in range(2)]
        zs, b_zs = R3.alloc("zs", [D], BF16)
        CBm = [R4.alloc("CBm%d" % d, [2, 128], BF16) for d in range(2)]
        C3s = [R4.alloc("C3_%d" % d, [256], BF16, parts=96) for d in range(2)]
        r1, b_r1 = R3.alloc("r1", [256], F32, parts=96)
        m2, b_m2 = R3.alloc("m2", [256], BF16, parts=96)
        r2, b_r2 = R3.alloc("r2", [256], F32, parts=96)
        la_flat = la_full.rearrange("p a b -> p (a b)")
        bE_flat = bE_full.rearrange("p a b -> p (a b)")
        rep3 = [[R4.alloc("rep3_%d_%d" % (d, w_), [96], F32) for w_ in range(2)] for d in range(2)]
        Es = [R4.alloc("E%d" % d, [16, 128], BF16) for d in range(2)]
        M = [R4.alloc("M%d" % d, [16, 128], BF16) for d in range(2)]
        Mb = [[Buf("M%d_%d" % (d, g)) for g in range(2)] for d in range(2)]
        tmpAs = [R4.alloc("tmpA%d" % d, [D], BF16) for d in range(2)]
        yvs = [R4.alloc("yv%d" % i, [D], F32) for i in range(2)]
        yns = [R4.alloc("yn%d" % i, [D], BF16) for i in range(1)] * 2
        ysT = [R4.alloc("ysT%d" % i, [8, 128], BF16) for i in range(1)] * 2
        tris = (trif, trib)

        def p5_front(it, t):
            ac = acol(t)
            for d in range(2):
                S.dma("sp", Sp[d][0], sst_d[b, d, t], [b_sstd[d][t]], [Sp[d][1]])
            for d in range(2):
                pq, pqb = next_bank()
                off = t * 32 + d * 16
                lap, b_lap = rep3[d][0]
                bep, b_bep = rep3[d][1]
                S.op("pool", lambda e: e.tensor_copy(out=lap.rearrange("p (r c) -> p r c", r=3), in_=la_flat[:, off:off + 32].unsqueeze(1).broadcast_to([128, 3, 32])),
                     [b_la], [b_lap])
                S.op("pool", lambda e: e.tensor_copy(out=bep.rearrange("p (r c) -> p r c", r=3), in_=bE_flat[:, off:off + 32].unsqueeze(1).broadcast_to([128, 3, 32])),
                     [b_bE], [b_bep])
                mm(pq[0:96, 0:128], lap, tris[d], True, True, [b_lap, b_const], [pqb])
                mm(pq[0:96, 128:256], bep, identF, True, True, [b_bep, b_const], [pqb])
                C3, b_C3 = C3s[d]
                S.op("act", lambda e: e.activation(out=C3, in_=pq[0:96, 0:256], func=AF.Identity), [pqb], [b_C3])
                S.op("dve", lambda e: e.tensor_tensor(out=r1[0:96, :], in0=pq[0:96, 0:256], in1=C3[0:96, :], op=ALU.subtract), [pqb, b_C3], [b_r1])
                S.op("act", lambda e: e.activation(out=C3[32:64, :], in_=r1[32:64, :], func=AF.Identity), [b_r1], [b_C3])
                S.op("act", lambda e: e.activation(out=m2[64:96, :], in_=r1[64:96, :], func=AF.Identity), [b_r1], [b_m2])
                S.op("dve", lambda e: e.tensor_tensor(out=r2[64:96, :], in0=r1[64:96, :], in1=m2[64:96, :], op=ALU.subtract), [b_r1, b_m2], [b_r2])
                S.op("act", lambda e: e.activation(out=C3[64:96, :], in_=r2[64:96, :], func=AF.Identity), [b_r2], [b_C3])
            pc, pcb = next_bank()
            for g in range(2):
                mm(pc[:, g * 128:(g + 1) * 128], BT[:, g, ac:ac + 128], CT[:, g, ac:ac + 128], True, True, [b_BT, b_CT], [pcb])
            S.op("act", lambda e: e.activation(out=CBm[0][0], in_=pc[:, 0:256].rearrange("p (g q) -> p g q", g=2), func=AF.Identity), [pcb], [CBm[0][1]])
            yield
            for d in range(2):
                po, pob = next_dbl()
                for g in range(2):
                    mm(po[:, g * 512:(g + 1) * 512], CT[:, g, ac:ac + 128], Sp[d][0][:, g * 512:(g + 1) * 512], True, True, [b_CT, Sp[d][1]], [pob[g]])
                dqv = dq[:, t, d * 16:(d + 1) * 16].unsqueeze(2).broadcast_to([128, 16, 64])
                S.op("dve", lambda e: e.tensor_tensor(out=tmpAs[d][0].rearrange("p (h c) -> p h c", h=16), in0=po.rearrange("p (h c) -> p h c", h=16),
                                                       in1=dqv, op=ALU.mult), pob + [b_dq], [tmpAs[d][1]])
            yield
            for d in range(2):
                E, b_E = Es[d]
                C3, b_C3 = C3s[d]
                for h4 in range(4):
                    pe_, peb = next_bank()
                    mm(pe_[:, 0:512], C3[:, 128:256], sel3[:, h4 * 4:(h4 + 1) * 4, :], True, False, [b_C3, b_const], [peb])
                    mm(pe_[:, 0:512], ident, negm[:, d, :].unsqueeze(1).broadcast_to([128, 4, 128]), False, False, [b_const], [peb])
                    for hh in range(4):
                        h = h4 * 4 + hh
                        mm(pe_[:, hh * 128:(hh + 1) * 128], sel3[:, h, :], C3[:, 0:128], False, hh == 3, [b_C3, b_const], [peb])
                    S.op("act", lambda e: e.activation(out=E[:, h4 * 4:(h4 + 1) * 4, :], in_=pe_[:, 0:512].rearrange("p (h q) -> p h q", h=4), func=AF.Exp),
                         [peb], [b_E])
                for g in range(2):
                    S.op("dve",
                         lambda e: e.tensor_tensor(out=M[d][0][:, g * 8:(g + 1) * 8, :], in0=E[:, g * 8:(g + 1) * 8, :],
                                                   in1=CBm[0][0][:, g, :].unsqueeze(1).broadcast_to([128, 8, 128]), op=ALU.mult),
                         [b_E, CBm[0][1]], [Mb[d][g]])
                yield
            pyd, pydb = next_dbl()
            for h in range(16):
                o = pyd[:, h * 64:(h + 1) * 64]
                mm(o, M[0][0][:, h, :], xs_tok[:, t, h * 64:(h + 1) * 64], True, False, [Mb[0][h // 8], b_xs], [pydb[h // 8]])
                mm(o, M[1][0][:, h, :], xs_tok[:, t, h * 64:(h + 1) * 64], False, True, [Mb[1][h // 8], b_xs], [pydb[h // 8]])
            yv, b_yv = yvs[it % 2]
            S.op("dve", lambda e: e.tensor_tensor(out=yv, in0=pyd, in1=tmpAs[0][0], op=ALU.add), pydb + [tmpAs[0][1]], [b_yv])
            S.op("dve", lambda e: e.tensor_tensor(out=yv, in0=yv, in1=tmpAs[1][0], op=ALU.add), [b_yv, tmpAs[1][1]], [b_yv])

        def p5_mid(it, t):
            yv, b_yv = yvs[it % 2]
            yn, b_yn = yns[0]
            pz, pzb = next_dbl()
            for hf in range(2):
                acc(pz[:, hf * 512:(hf + 1) * 512], [pzb[hf]], [(hT[:, kc, tcol(t):tcol(t) + 128], wz[:, kc, hf * 512:(hf + 1) * 512], [wzb, b_hT[t]]) for kc in range(8)])
            S.op("act", lambda e: e.activation(out=zs, in_=pz, func=AF.Silu), pzb, [b_zs])
            yield
            dsv = dsk_bc[:, l, :].unsqueeze(2).broadcast_to([128, 16, 64])
            S.op("pool", lambda e: e.tensor_tensor(out=yn.rearrange("p (h c) -> p h c", h=16), in0=xs_tok[:, t, :].rearrange("p (h c) -> p h c", h=16),
                                                    in1=dsv, op=ALU.mult), [b_xs, b_const], [b_yn])
            S.op("pool", lambda e: e.tensor_tensor(out=yv, in0=yv, in1=yn, op=ALU.add), [b_yv, b_yn], [b_yv])
            S.op("dve", lambda e: e.tensor_tensor(out=yv, in0=yv, in1=zs, op=ALU.mult), [b_yv, b_zs], [b_yv])
            yield
            for g in range(2):
                S.op("act", lambda e: e.activation(out=yn[:, g * 512:(g + 1) * 512], in_=yv[:, g * 512:(g + 1) * 512], func=AF.Square, accum_out=ss[:, g:g + 1]), [b_yv], [b_yn, b_ss])
            for g in range(2):
                rstd_from_ss(g, 512)
            for g in range(2):
                S.op("dve", lambda e: e.tensor_scalar(out=yn[:, g * 512:(g + 1) * 512], in0=yv[:, g * 512:(g + 1) * 512], scalar1=rstd[:, g:g + 1],
                                                       scalar2=None, op0=ALU.mult), [b_yv, b_rstd], [b_yn])

        def p5_back(it, t):
            yn, b_yn = yns[it % 2]
            pa, pbs = next_dbl()
            pT = pa[:, 0:512].bitcast(BF16)
            for dc in range(8):
                tr(pT[:, dc * 128:(dc + 1) * 128], yn[:, dc * 128:(dc + 1) * 128], [b_yn], [pbs[0]])
            yo, yob = ysT[it % 2]
            for dc in range(8):
                if dc % 2 == 0:
                    S.op("act", lambda e: e.activation(out=yo[:, dc, :], in_=pT[:, dc * 128:(dc + 1) * 128], func=AF.Identity, scale=snT[:, l, dc:dc + 1]), [pbs[0], b_const], [yob])
                else:
                    S.op("dve", lambda e: e.tensor_scalar(out=yo[:, dc, :], in0=pT[:, dc * 128:(dc + 1) * 128], scalar1=snT[:, l, dc:dc + 1], scalar2=None, op0=ALU.mult),
                         [pbs[0], b_const], [yob])
            S.dma("act", ys_d[b, t].rearrange("p (c n) -> p c n", c=8), yo, [yob], [b_ysd], append=True)

        def p5_tail(it, t):
            yield from p5_mid(it, t)
            yield
            p5_back(it, t)

        for _ in p5_front(0, tiles_y[0]):
            pass
        for it, t in enumerate(tiles_y):
            fg = p5_front(it + 1, tiles_y[it + 1]) if it + 1 < len(tiles_y) else iter(())
            tg = p5_tail(it, t)
            fdone = tdone = False
            while not (fdone and tdone):
                if not fdone:
                    try:
                        next(fg)
                    except StopIteration:
                        fdone = True
                if not tdone:
                    try:
                        next(tg)
                    except StopIteration:
                        tdone = True
        S.barrier()
        S.mark("P6 b%d l%d" % (b, l))
        R3.reset()
        R4.reset()
        AB, b_AB = R3.alloc("AB", [NT, 4, 256], BF16)
        fT, b_fT = R4.alloc("fT", [4, 512], BF16)
        clb = [R4.alloc("clb%d" % i, [2, 2, 512], BF16) for i in range(3)]
        yfo = [R4.alloc("yfo%d" % i, [4, 4, 128], BF16) for i in range(2)]
        wf, wfb = wload(wq_f[l, 0].rearrange("p (k w) -> p k w", k=8), [8, 512], "wf", l=l)
        groups = GROUPS if l == 0 else GROUPS[1:]
        for (t0, ntl) in groups:
            n = ntl * 128
            c0 = tcol(t0)
            for g in range(4):
                pa, pb_ = next_bank()
                acc(pa[:, 0:n], [pb_], [(wf[:, kc, g * 128:(g + 1) * 128], hT[:, kc, c0:c0 + n], [wfb] + b_hT[t0:t0 + ntl]) for kc in range(8)])
                if g % 2 == 0:
                    S.op("act", lambda e: e.activation(out=fT[:, g, 0:n], in_=pa[:, 0:n], func=AF.Identity), [pb_], [b_fT])
                else:
                    S.op("dve", lambda e: e.tensor_copy(out=fT[:, g, 0:n], in_=pa[:, 0:n]), [pb_], [b_fT])
            for i in range(ntl):
                t = t0 + i
                pa, pbs = next_dbl()
                for g in range(4):
                    mm(pa[:, g * 256:(g + 1) * 256], fT[:, g, i * 128:(i + 1) * 128], cs, True, True, [b_fT, b_const], [pbs[g // 2]])
                if i % 2 == 0:
                    S.op("act", lambda e: e.activation(out=AB[:, t], in_=pa.rearrange("p (g c) -> p g c", g=4), func=AF.Identity), pbs, [b_AB])
                else:
                    S.op("dve", lambda e: e.tensor_copy(out=AB[:, t], in_=pa.rearrange("p (g c) -> p g c", g=4)), pbs, [b_AB])
        k_cl = 0
        k_yf = 0
        segs = [(2, 16, cst_in, 512, 4)]
        if l == 0:
            segs.append((0, 2, cst256_in, 256, 1))
        for (tb, ntt, cst_t, kw, nkb) in segs:
            for kb in range(nkb):
                pas = [next_bank() for g in range(4)]
                for tt2 in range(ntt // 2):
                    ca, cbuf = clb[k_cl % 3]
                    k_cl += 1
                    S.dma("sp", ca[:, :, :, 0:kw], cst_t[kb, tt2].rearrange("p (a c k) -> p a c k", a=2, c=2), [], [cbuf])
                    for a_ in range(2):
                        tt = tt2 * 2 + a_
                        for g in range(4):
                            mm(pas[g][0][:, 0:kw], AB[:, tb + tt, g, 0:128], ca[:, a_, 0, 0:kw], tt == 0, False, [b_AB, cbuf], [pas[g][1]])
                            mm(pas[g][0][:, 0:kw], AB[:, tb + tt, g, 128:256], ca[:, a_, 1, 0:kw], False, tt == ntt - 1, [b_AB, cbuf], [pas[g][1]])
                yo, yob = yfo[k_yf % 2]
                k_yf += 1
                ntile = kw // 128
                for g in range(4):
                    src_ = pas[g][0][:, 0:kw].rearrange("p (t n) -> p t n", t=ntile)
                    if g % 2 == 0:
                        S.op("act", lambda e: e.activation(out=yo[:, 0:ntile, g, :], in_=src_, func=AF.Identity), [pas[g][1]], [yob])
                    else:
                        S.op("dve", lambda e: e.tensor_copy(out=yo[:, 0:ntile, g, :], in_=src_), [pas[g][1]], [yob])
                tq = tb + kb * 4
                S.dma("act", yf_d[b, tq:tq + ntile].rearrange("t p (g n) -> p t g n", g=4), yo[:, 0:ntile], [yob], [b_yfd], append=True)
        S.barrier()

    b_ysd = Buf("ysd")
    b_yfd = Buf("yfd")
    b_ygd = Buf("ygd")
    b_xres = Buf("xres")
    b_out = Buf("out")

    wp4_buf = [Buf("wp4_%d" % i) for i in range(4)]
    wp4_next = [0]
    WP4 = [Region(arena, WPB + i * 12288, 12288) for i in range(4)]

    def wl4(shape, srcs, wkey):
        i = wp4_next[0]
        wp4_next[0] = (i + 1) % 4
        WP4[i].reset()
        ap, _ = WP4[i].alloc("w4", shape, BF16)
        for k, (sl, src) in enumerate(srcs):
            S.dma("sp", sl(ap), src, [b_wqd[wkey]], [wp4_buf[i]], append=(k > 0))
        return ap, wp4_buf[i]

    def phaseII(b, l):
        last = (l == NL - 1)
        S.mark("PII b%d l%d" % (b, l))
        groups = GROUPS if not last else GROUPS[1:]
        R3.reset()
        R3.off = 16384
        R4.reset()
        r3o = R3.base + R3.off
        yT, b_yT = R3.alloc("yT", [4, 16, 128], BF16)
        gT_extra, _ = R3.alloc("gTx", [6, 512], BF16)
        gT = arena[:, r3o // 2:r3o // 2 + 22 * 512].rearrange("p (a b) -> p a b", a=22)
        b_yTp = [Buf("yT_p%d" % i) for i in range(3)]
        b_gTl = b_yTp
        mT, b_mT = R3.alloc("mT", [8, 512], BF16)
        xt, _ = R3.alloc("xt", [4, D], F32)
        b_xt = [Buf("xt%d" % i) for i in range(4)]
        h2T, b_h2T = R3.alloc("h2T", [8, 512], BF16)
        g1_bc, b_g1 = R4.alloc("g1_bc", [D], F32)
        g2_bc, b_g2 = R4.alloc("g2_bc", [D], F32)
        sigs = [R4.alloc("sig%d" % i, [512], F32) for i in range(2)]
        tmpm, b_tmpm = R4.alloc("tmpm", [512], F32)
        macc, b_macc = R4.alloc("macc", [512], F32)
        tmpx, b_tmpx = R4.alloc("tmpx", [D], F32)
        xns = [R4.alloc("xn%d" % i, [D], BF16) for i in range(2)]
        accas = [R4.alloc("acca%d" % i, [512], F32) for i in range(2)]
        accvs = [R4.alloc("accv%d" % i, [512], F32) for i in range(2)]
        sas = [R4.alloc("sa%d" % i, [512], F32) for i in range(1)] * 2
        cur_bi = [None]
        k_sig = [0]
        k_pair = 0
        pendE = [None]

        def merge_w(jb):
            wg = wl4([24, 256], [((lambda a: a), wq_g[l, jb].rearrange("p (k w) -> p k w", k=24))], ("in", l))
            wbr = wl4([16, 256], [((lambda a: a), wq_mo[l, jb].rearrange("p (k w) -> p k w", k=16))], ("mo", l))
            return wg, wbr

        for (t0, ntl) in groups:
            n = ntl * 128
            c0 = tcol(t0)
            bi = NB if t0 == 0 else b
            if cur_bi[0] != bi:
                cur_bi[0] = bi
                S.dma("sp", g1_bc, modrow_d[l, bi, 2 * D:3 * D].partition_broadcast(128), [b_modrow], [b_g1])
                S.dma("sp", g2_bc, modrow_d[l, bi, 5 * D:6 * D].partition_broadcast(128), [b_modrow], [b_g2])
            mw0 = merge_w(0)
            S.dma("sp", yT[:, 0:ntl, 0:8, :], ys_d[b, t0:t0 + ntl].rearrange("t p (c n) -> p t c n", c=8), [b_ysd], b_yTp)
            S.dma("sp", yT[:, 0:ntl, 8:12, :], yf_d[b, t0:t0 + ntl].rearrange("t p (c n) -> p t c n", c=4), [b_yfd], [b_yTp[1]], append=True)
            S.dma("sp", yT[:, 0:ntl, 12:16, :], yg_d[b, t0:t0 + ntl].rearrange("t p (c n) -> p t c n", c=4), [b_ygd], [b_yTp[2]], append=True)

            S.mark("IIa g%d b%d l%d" % (t0, b, l))

            def gen_A(t0=t0, ntl=ntl, n=n, c0=c0, mw=mw0):
                for jb in range(4):
                    (wg, wgb), (wbr, wbb) = mw
                    if jb < 3:
                        mw = merge_w(jb + 1)
                    for jj in range(2):
                        j = jb * 2 + jj
                        krange = [(0, 8), (8, 12), (12, 16)]
                        for br in range(3):
                            pg, pgb = next_bank()
                            acc(pg[:, 0:n], [pgb], [(wg[:, br * 8 + kc, jj * 128:(jj + 1) * 128], hT[:, kc, c0:c0 + n], [wgb] + b_hT[t0:t0 + ntl]) for kc in range(8)])
                            sig, b_sig = sigs[k_sig[0] % 2]
                            k_sig[0] += 1
                            S.op("act", lambda e: e.activation(out=sig[:, 0:n], in_=pg[:, 0:n], func=AF.Sigmoid), [pgb], [b_sig])
                            pp, ppb = next_bank()
                            k0, k1 = krange[br]
                            acc(pp[:, 0:n], [ppb], [(wbr[:, kc, jj * 128:(jj + 1) * 128], yT[:, 0:ntl, kc, :], [wbb, b_yTp[br]]) for kc in range(k0, k1)])
                            if br == 0:
                                S.op("dve", lambda e: e.tensor_tensor(out=macc[:, 0:n], in0=pp[:, 0:n], in1=sig[:, 0:n], op=ALU.mult), [ppb, b_sig], [b_macc])
                            else:
                                S.op("dve", lambda e: e.tensor_tensor(out=tmpm[:, 0:n], in0=pp[:, 0:n], in1=sig[:, 0:n], op=ALU.mult), [ppb, b_sig], [b_tmpm])
                                if br == 1:
                                    S.op("dve", lambda e: e.tensor_tensor(out=macc[:, 0:n], in0=macc[:, 0:n], in1=tmpm[:, 0:n], op=ALU.add), [b_macc, b_tmpm], [b_macc])
                                else:
                                    S.op("dve", lambda e: e.tensor_tensor(out=mT[:, j, 0:n], in0=macc[:, 0:n], in1=tmpm[:, 0:n], op=ALU.add), [b_macc, b_tmpm], [b_mT])
                            yield

            ga = gen_A()
            if pendE[0] is not None:
                ge = pendE[0]
                pendE[0] = None
                edone = False
                while not edone:
                    try:
                        next(ge)
                    except StopIteration:
                        edone = True
                    for _ in range(3):
                        next(ga, None)
            for i in range(ntl):
                t = t0 + i
                if l == 0:
                    src = xrow(b, t)
                    rb = []
                else:
                    src = xres_d[b, (t - 2) * 128:(t - 1) * 128, :]
                    rb = [b_xres]
                S.dma("sp", xt[:, i, :], src, rb, [b_xt[i]])
            for _ in ga:
                pass
            S.mark("IIb g%d b%d l%d" % (t0, b, l))
            wos = [wl4([8, 512], [((lambda a: a), wq_out[l, hf].rearrange("p (k w) -> p k w", k=8))], ("out", l)) for hf in range(2)]
            wus = {}

            def up_w(pb6):
                ncol = (4 if pb6 < 5 else 2) * 128
                wa = wl4([8, ncol], [((lambda a: a), wq_ua[l, pb6][:, 0:8 * ncol].rearrange("p (k w) -> p k w", k=8))], ("up", l))
                wv = wl4([8, ncol], [((lambda a: a), wq_uvv[l, pb6][:, 0:8 * ncol].rearrange("p (k w) -> p k w", k=8))], ("up", l))
                return wa, wv

            wus[0] = up_w(0)
            def wout_mm(i):
                po, pob = next_dbl()
                for hf in range(2):
                    acc(po[:, hf * 512:(hf + 1) * 512], [pob[hf]], [(mT[:, kc, i * 128:(i + 1) * 128], wos[hf][0][:, kc, :], [wos[hf][1], b_mT]) for kc in range(8)])
                return po, pob

            pend = wout_mm(0)
            for i in range(ntl):
                po, pob = pend
                if i + 1 < ntl:
                    pend = wout_mm(i + 1)
                S.op("dve", lambda e: e.tensor_tensor(out=tmpx, in0=po, in1=g1_bc, op=ALU.mult), pob + [b_g1], [b_tmpx])
                S.op("dve", lambda e: e.tensor_tensor(out=xt[:, i, :], in0=xt[:, i, :], in1=tmpx, op=ALU.add), [b_xt[i], b_tmpx], [b_xt[i]])
                norm_front(xt[:, i, :], [b_xt[i]], *xns[i % 2])
                if i >= 1:
                    norm_back(xns[(i - 1) % 2][0], xns[(i - 1) % 2][1], l, bi, 2, h2T, [b_h2T], (i - 1) * 128)
            norm_back(xns[(ntl - 1) % 2][0], xns[(ntl - 1) % 2][1], l, bi, 2, h2T, [b_h2T], (ntl - 1) * 128)
            S.mark("IIc g%d b%d l%d" % (t0, b, l))
            if t0 == 0:
                R_, W_ = 1, 256
            else:
                R_, W_ = ntl * 2, 64
            for pb6 in range(6):
                npair = 4 if pb6 < 5 else 2
                (wa, wab), (wv, wvb) = wus[pb6]
                if pb6 < 5:
                    wus[pb6 + 1] = up_w(pb6 + 1)
                for pp_ in range(npair):
                    p = pb6 * 4 + pp_
                    pa_, pab = next_bank()
                    acc(pa_[:, 0:n], [pab], [(wa[:, kc, pp_ * 128:(pp_ + 1) * 128], h2T[:, kc, 0:n], [wab, b_h2T]) for kc in range(8)])
                    pv_, pvb = next_bank()
                    acc(pv_[:, 0:n], [pvb], [(wv[:, kc, pp_ * 128:(pp_ + 1) * 128], h2T[:, kc, 0:n], [wvb, b_h2T]) for kc in range(8)])
                    acca, b_acca = accas[k_pair % 2]
                    accv, b_accv = accvs[k_pair % 2]
                    sa_, b_sa = sas[k_pair % 2]
                    k_pair += 1
                    for (ps_, psb, ac_, bb_a, ch) in ((pa_, pab, acca, b_acca, p), (pv_, pvb, accv, b_accv, 22 + p)):
                        S.op("act", lambda e: e.activation(out=ac_[:, 0:n], in_=ps_[:, 0:n], func=AF.Identity, scale=fcwT[:, l, ch, 1:2], bias=fcbT[:, l, ch:ch + 1]),
                             [psb, b_const], [bb_a])
                        a3 = ac_[:, 0:n].rearrange("p (r w) -> p r w", r=R_)
                        p3 = ps_[:, 0:n].rearrange("p (r w) -> p r w", r=R_)
                        S.op("dve", lambda e: e.scalar_tensor_tensor(out=a3[:, :, 1:W_], in0=p3[:, :, 0:W_ - 1], scalar=fcwT[:, l, ch, 0:1], in1=a3[:, :, 1:W_],
                                                                      op0=ALU.mult, op1=ALU.add), [psb, b_const, bb_a], [bb_a])
                        S.op("dve", lambda e: e.scalar_tensor_tensor(out=a3[:, :, 0:W_ - 1], in0=p3[:, :, 1:W_], scalar=fcwT[:, l, ch, 2:3], in1=a3[:, :, 0:W_ - 1],
                                                                      op0=ALU.mult, op1=ALU.add), [psb, b_const, bb_a], [bb_a])
                    S.op("act", lambda e: e.activation(out=sa_[:, 0:n], in_=acca[:, 0:n], func=AF.Silu), [b_acca], [b_sa])
                    S.op("pool", lambda e: e.tensor_tensor(out=gT[:, p, 0:n], in0=accv[:, 0:n], in1=sa_[:, 0:n], op=ALU.mult), [b_accv, b_sa], b_gTl)
            S.mark("IId g%d b%d l%d" % (t0, b, l))
            wds = {0: wl4([22, 256], [((lambda a: a), wq_dn[l, 0].rearrange("p (k w) -> p k w", k=22))], ("dn", l))}
            for qt in range(4):
                wd, wdb = wds[qt]
                if qt < 3:
                    wds[qt + 1] = wl4([22, 256], [((lambda a: a), wq_dn[l, qt + 1].rearrange("p (k w) -> p k w", k=22))], ("dn", l))
                for i in range(ntl):
                    pd_, pdb = next_bank()
                    acc(pd_[:, 0:256], [pdb], [(gT[:, p, i * 128:(i + 1) * 128], wd[:, p, :], [wdb] + b_gTl) for p in range(22)])
                    S.op("dve", lambda e: e.tensor_tensor(out=tmpx[:, 0:256], in0=pd_[:, 0:256], in1=g2_bc[:, qt * 256:(qt + 1) * 256], op=ALU.mult), [pdb, b_g2], [b_tmpx])
                    S.op("dve", lambda e: e.tensor_tensor(out=xt[:, i, qt * 256:(qt + 1) * 256], in0=xt[:, i, qt * 256:(qt + 1) * 256], in1=tmpx[:, 0:256], op=ALU.add),
                         [b_xt[i], b_tmpx], [b_xt[i]])
            S.mark("IIe g%d b%d l%d" % (t0, b, l))

            def gen_E(t0=t0, ntl=ntl, bi=bi):
                for i in range(ntl):
                    t = t0 + i
                    if not last:
                        if t >= 2:
                            S.dma("pool", xres_d[b, (t - 2) * 128:(t - 1) * 128, :], xt[:, i, :], [b_xt[i]], [b_xres], append=True)
                        norm_front(xt[:, i, :], [b_xt[i]], *xns[i % 2])
                        yield
                        if i >= 1:
                            norm_back(xns[(i - 1) % 2][0], xns[(i - 1) % 2][1], l + 1, bi, 1, hT, [b_hT[t - 1]], tcol(t - 1))
                            yield
                        if i == ntl - 1:
                            norm_back(xns[i % 2][0], xns[i % 2][1], l + 1, bi, 1, hT, [b_hT[t]], tcol(t))
                            yield
                    else:
                        junk, b_junk = xns[i % 2]
                        S.op("dve", lambda e: e.scalar_tensor_tensor(out=junk, in0=xt[:, i, :], scalar=1.0, in1=xt[:, i, :], op0=ALU.mult, op1=ALU.mult, accum_out=ss[:, 0:1]),
                             [b_xt[i]], [b_junk, b_ss], nsl=2)
                        rstd_from_ss(0, D)
                        S.op("dve", lambda e: e.scalar_tensor_tensor(out=tmpx, in0=xt[:, i, :], scalar=rstd[:, 0:1], in1=fw_bc, op0=ALU.mult, op1=ALU.mult),
                             [b_xt[i], b_rstd, b_const], [b_tmpx])
                        S.dma("pool", out_d[b, (t - 2) * 128:(t - 1) * 128, :], tmpx, [b_tmpx], [b_out], append=True)
                        yield

            pendE[0] = gen_E()
        if pendE[0] is not None:
            for _ in pendE[0]:
                pass
            pendE[0] = None
        S.barrier()

    for b in range(NB):
        phase1(b)
        for l in range(NL):
            phaseI(b, l)
            phaseII(b, l)
    S.barrier()
    S.mark("end")
    return nc, S


def _consts():
    bf = ml_dtypes.bfloat16
    k = np.arange(128)
    c = {}
    c["ident"] = np.eye(128, dtype=np.float32).astype(bf)
    c["trif"] = (k[:, None] <= k[None, :]).astype(np.float32)
    c["trib"] = (k[:, None] >= k[None, :]).astype(np.float32)
    c["ones"] = np.ones((128, 128), np.float32)
    sel3 = np.zeros((96, 16, 128), np.float32)
    for h in range(16):
        for r in (0, 32, 64):
            sel3[r + h, h, :] = 1.0
    c["sel3"] = sel3.astype(bf)
    c["identF"] = np.eye(128, dtype=np.float32)
    negm = np.zeros((128, 2, 128), np.float32)
    negm[:, 0, :] = np.where(k[:, None] > k[None, :], -30000.0, 0.0)
    negm[:, 1, :] = np.where(k[:, None] < k[None, :], -30000.0, 0.0)
    c["negm"] = negm.astype(bf)
    ang = 2 * np.pi * ((k[:, None] * k[None, :]) % 128) / 128.0
    c["cs"] = (np.concatenate([np.cos(ang), np.sin(ang)], axis=1) / np.sqrt(128.0)).astype(bf)
    for L, sfx, kw in ((SEQ, "", 512), (CTXL, "256", 256)):
        t = np.arange(L, dtype=np.int64)
        a = 2 * np.pi * ((t[:, None] * t[None, :]) % L).astype(np.float64) / L
        tab = np.stack([np.cos(a), -np.sin(a)], axis=0) / np.sqrt(L)
        nkb, ntt2 = L // kw, L // 256
        tab = tab.reshape(2, ntt2, 2, 128, nkb, kw)
        tab = np.transpose(tab, (4, 1, 3, 2, 0, 5))
        c["cst" + sfx] = np.ascontiguousarray(tab.reshape(nkb, ntt2, 128, 4 * kw)).astype(np.float32).astype(bf)
    return c


_CACHE = {}


def _pT(v):
    v = np.asarray(v, np.float32)
    lead = v.shape[:-1]
    n = v.shape[-1] // 128
    r = v.reshape(lead + (n, 128))
    return np.ascontiguousarray(np.moveaxis(r, -1, 0))


def kernel(**inp):
    NB = 32 // NCORES
    if "nc" not in _CACHE:
        _CACHE["nc"] = build(NB)[0]
        _CACHE["consts"] = _consts()
    nc = _CACHE["nc"]
    f = lambda a: np.ascontiguousarray(np.asarray(a, np.float32))
    shared = dict(_CACHE["consts"])
    NL = 2
    shared["w_mod"] = f(inp["w_mod"])
    shared["b_mod"] = f(inp["b_mod"])
    shared["bmodT"] = _pT(inp["b_mod"])
    shared["n1T"] = _pT(inp["norm1_w"])
    shared["n2T"] = _pT(inp["norm2_w"])
    shared["final_norm_w"] = f(inp["final_norm_w"])
    shared["w_in"] = f(inp["w_in"])
    cw = np.asarray(inp["ssd_conv_w"], np.float32)
    shared["cwT"] = np.ascontiguousarray(np.transpose(cw.reshape(NL, 5, 12, 128), (3, 0, 2, 1)))
    shared["cbT"] = _pT(inp["ssd_conv_b"])
    shared["ssd_a_log"] = f(inp["ssd_a_log"]).reshape(NL, 32)
    shared["ssd_dt_bias"] = f(inp["ssd_dt_bias"]).reshape(NL, 32)
    shared["ssd_d"] = f(inp["ssd_d"])
    shared["snT"] = _pT(inp["ssd_norm_w"])
    ws = np.asarray(inp["gmlp_w_s"], np.float32)
    shared["wsT"] = np.ascontiguousarray(np.transpose(ws, (0, 3, 1, 2)))
    shared["gmlp_b_s"] = f(inp["gmlp_b_s"]).reshape(NL, 512)
    shared["w_ssd_o"] = f(inp["w_ssd_o"])
    shared["w_fft_o"] = f(inp["w_fft_o"])
    shared["w_gmlp_o"] = f(inp["w_gmlp_o"])
    shared["w_out"] = f(inp["w_out"])
    shared["ffn_w_up"] = f(inp["ffn_w_up"])
    fw = np.asarray(inp["ffn_conv_w"], np.float32)
    shared["fcwT"] = np.ascontiguousarray(np.transpose(fw.reshape(NL, 3, 44, 128), (3, 0, 2, 1)))
    shared["fcbT"] = _pT(inp["ffn_conv_b"])
    shared["ffn_w_down"] = f(inp["ffn_w_down"])
    x = f(inp["x"])
    ctx = f(inp["ctx"])
    c = f(inp["c"])
    cctx = f(inp["c_ctx"])
    in_maps = []
    for i in range(NCORES):
        m = dict(shared)
        m["x"] = x[i * NB:(i + 1) * NB]
        m["ctx"] = ctx[i * NB:(i + 1) * NB]
        cc = np.concatenate([c[i * NB:(i + 1) * NB], cctx[None, :]], axis=0)
        m["ccT"] = np.ascontiguousarray(np.transpose(cc.reshape(NB + 1, 8, 128), (2, 1, 0)))
        in_maps.append(m)
    res = run_bass_kernel_spmd(nc, in_maps, core_ids=list(range(NCORES)))
    return np.concatenate([r["out"] for r in res.results], axis=0).astype(np.float32)
```

```python
import numpy as np
import ml_dtypes
import concourse.bass as bass
import concourse.mybir as mybir
from concourse.bass_utils import run_bass_kernel_spmd
from contextlib import ExitStack

F32 = mybir.dt.float32
BF16 = mybir.dt.bfloat16
AF = mybir.ActivationFunctionType
ALU = mybir.AluOpType

D = 1024
SEQ = 2048
CTXL = 256
NT = 18
TOK = 2304
DFF = 2816
OFF_DT = 1536
OFF_Z = 1568
OFF_FFT = 2592
OFF_GMLP = 3104
OFF_GATE = 4128
IN_COLS = 7200
EPS = 1e-6
NCORES = 8
SAME_ENGINE_SYNC = True


class Buf:
    __slots__ = ("name", "w", "r")

    def __init__(self, name):
        self.name = name
        self.w = {}
        self.r = {}


class Sched:
    ND = 12

    def __init__(self, nc, es):
        self.nc = nc
        self.eng = {"pe": nc.tensor, "act": nc.scalar, "dve": nc.vector, "pool": nc.gpsimd, "sp": nc.sync}
        self.sem = {e: es.enter_context(nc.semaphore("s_" + e)) for e in ("pe", "act", "dve", "pool")}
        self.cnt = {e: 0 for e in self.sem}
        self.dsem = {q: [es.enter_context(nc.semaphore("d_%s_%d" % (q, i))) for i in range(self.ND)] for q in ("sp", "pool", "act")}
        self.dval = {q: [0] * self.ND for q in ("sp", "pool", "act")}
        self.dlast = {q: [None] * self.ND for q in ("sp", "pool", "act")}
        self.dnext = {"sp": 0, "pool": 0, "act": 0}
        self.waited = {e: {} for e in self.eng}
        self.nops = 0
        self.marks = []
        self.nsl = {e: 0 for e in self.sem}

    def mark(self, name):
        self.marks.append((name, dict(self.nsl)))

    def ensure(self, e, tok):
        if tok is None:
            return
        key, val, sem = tok
        if key == e and (e == "pe" or not SAME_ENGINE_SYNC):
            return
        w = self.waited[e]
        if w.get(key, 0) >= val:
            return
        self.eng[e].wait_ge(sem, val)
        w[key] = val

    def _deps(self, e, reads, writes, append=False):
        for b in reads:
            for t in b.w.values():
                self.ensure(e, t)
        for b in writes:
            if not append:
                for t in b.w.values():
                    self.ensure(e, t)
            for t in b.r.values():
                self.ensure(e, t)

    def _mark(self, tok, reads, writes, append=False):
        k = tok[0]
        for b in reads:
            b.r[k] = tok
        for b in writes:
            if append:
                b.w[k] = tok
            else:
                b.w = {k: tok}
            b.r = {}

    def op(self, e, fn, reads=(), writes=(), nsl=1):
        self._deps(e, reads, writes)
        ins = fn(self.eng[e])
        self.cnt[e] += 1
        self.nsl[e] += nsl
        tok = (e, self.cnt[e], self.sem[e])
        ins.then_inc(self.sem[e], 1)
        self._mark(tok, reads, writes)
        self.nops += 1
        return tok

    def dma(self, q, out, in_, reads=(), writes=(), append=False, **kw):
        self._deps(q, reads, writes, append)
        i = self.dnext[q]
        self.dnext[q] = (i + 1) % self.ND
        self.ensure(q, self.dlast[q][i])
        ins = self.eng[q].dma_start(out=out, in_=in_, **kw)
        self.dval[q][i] += 16
        sem = self.dsem[q][i]
        ins.then_inc(sem, 16)
        tok = (("d", q, i), self.dval[q][i], sem)
        self.dlast[q][i] = tok
        self._mark(tok, reads, writes, append)
        self.nops += 1
        return tok

    def barrier(self, engines=("pe", "act", "dve", "pool", "sp")):
        toks = [(e, self.cnt[e], self.sem[e]) for e in self.sem if self.cnt[e] > 0]
        for q in self.dlast:
            toks += [t for t in self.dlast[q] if t is not None]
        for e in engines:
            for t in toks:
                self.ensure(e, t)


class Region:
    def __init__(self, arena, base, size):
        self.arena = arena
        self.base = base
        self.size = size
        self.off = 0

    def reset(self):
        self.off = 0

    def alloc(self, name, shape, dtype, parts=128):
        esz = 4 if dtype == F32 else 2
        n = 1
        for s in shape:
            n *= s
        nb = (n * esz + 63) // 64 * 64
        assert self.off + nb <= self.size, (name, self.off, nb, self.size)
        o = (self.base + self.off) // 2
        self.off += nb
        ap = self.arena[0:parts, o:o + n * esz // 2]
        if dtype == F32:
            ap = ap.bitcast(F32)
        if len(shape) == 2:
            ap = ap.rearrange("p (a b) -> p a b", a=shape[0])
        elif len(shape) == 3:
            ap = ap.rearrange("p (a b c) -> p a b c", a=shape[0], b=shape[1])
        return ap, Buf(name)


def build(NB, NL=2):
    NB1 = NB + 1
    nc = bass.Bass("TRN2", target_bir_lowering=False)
    es = ExitStack()

    def din(name, shape, dt=F32):
        return nc.dram_tensor(name, list(shape), dt, kind="ExternalInput").ap()

    def dscr(name, shape, dt):
        return nc.dram_tensor(name, list(shape), dt, kind="Internal").ap()

    x_in = din("x", [NB, SEQ, D])
    ctx_in = din("ctx", [NB, CTXL, D])
    ccT_in = din("ccT", [128, 8, NB1])
    w_mod = din("w_mod", [NL, D, 6 * D])
    bmod_row = din("b_mod", [NL, 6 * D])
    bmodT_in = din("bmodT", [128, NL, 48])
    n1T_in = din("n1T", [128, NL, 8])
    n2T_in = din("n2T", [128, NL, 8])
    fn_in = din("final_norm_w", [D])
    w_in = din("w_in", [NL, D, IN_COLS])
    cwT_in = din("cwT", [128, NL, 12, 5])
    cbT_in = din("cbT", [128, NL, 12])
    alog_in = din("ssd_a_log", [NL, 32])
    dtb_in = din("ssd_dt_bias", [NL, 32])
    dsk_in = din("ssd_d", [NL, 16])
    snT_in = din("snT", [128, NL, 8])
    wsT_in = din("wsT", [NL, 128, 4, 128])
    bs_in = din("gmlp_b_s", [NL, 512])
    w_sso = din("w_ssd_o", [NL, D, D])
    w_ffo = din("w_fft_o", [NL, 512, D])
    w_gmo = din("w_gmlp_o", [NL, 512, D])
    w_out = din("w_out", [NL, D, D])
    w_up = din("ffn_w_up", [NL, D, 2 * DFF])
    fcwT_in = din("fcwT", [128, NL, 44, 3])
    fcbT_in = din("fcbT", [128, NL, 44])
    w_dn = din("ffn_w_down", [NL, DFF, D])
    ident_in = din("ident", [128, 128], BF16)
    trif_in = din("trif", [128, 128])
    trib_in = din("trib", [128, 128])
    ones_in = din("ones", [128, 128])
    sel3_in = din("sel3", [96, 16, 128], BF16)
    identF_in = din("identF", [128, 128])
    negm_in = din("negm", [128, 2, 128], BF16)
    cs_in = din("cs", [128, 256], BF16)
    cst_in = din("cst", [4, 8, 128, 2 * 2 * 512], BF16)
    cst256_in = din("cst256", [1, 1, 128, 2 * 2 * 256], BF16)
    out_d = nc.dram_tensor("out", [NB, SEQ, D], F32, kind="ExternalOutput").ap()

    wq_x = dscr("wq_x", [NL, 3, 128, 8 * 512], BF16)
    wq_dt = dscr("wq_dt", [NL, 1, 128, 8 * 32], BF16)
    wq_z = dscr("wq_z", [NL, 1, 128, 8 * 1024], BF16)
    wq_f = dscr("wq_f", [NL, 1, 128, 8 * 512], BF16)
    wq_uv = dscr("wq_uv", [NL, 1, 128, 8 * 1024], BF16)
    wq_g = dscr("wq_g", [NL, 4, 128, 24 * 256], BF16)
    wq_mo = dscr("wq_mo", [NL, 4, 128, 16 * 256], BF16)
    wq_out = dscr("wq_out", [NL, 2, 128, 8 * 512], BF16)
    wq_ua = dscr("wq_ua", [NL, 6, 128, 8 * 512], BF16)
    wq_uvv = dscr("wq_uvv", [NL, 6, 128, 8 * 512], BF16)
    wq_dn = dscr("wq_dn", [NL, 4, 128, 22 * 256], BF16)
    modrow_d = dscr("modrow_d", [NL, NB1, 6 * D], F32)
    xres_d = dscr("xres_d", [NB, SEQ, D], F32)
    ys_d = dscr("ys_d", [NB, NT, 128, 1024], BF16)
    yf_d = dscr("yf_d", [NB, NT, 128, 512], BF16)
    yg_d = dscr("yg_d", [NB, NT, 128, 512], BF16)
    sst_d = dscr("sst_d", [NB, 2, NT, 128, D], BF16)

    S = Sched(nc, es)
    TOTB = 212000
    PERMSZ = 24064
    arena = es.enter_context(nc.sbuf_tensor("arena", [128, TOTB // 2], BF16))
    PERM = Region(arena, 0, PERMSZ)
    HT = Region(arena, PERMSZ, 36864)
    WPB = PERMSZ + 36864
    WP = [Region(arena, WPB + i * 16384, 16384) for i in range(2)]
    R3 = Region(arena, WPB + 32768, 73728)
    R4 = Region(arena, WPB + 32768 + 73728, TOTB - (WPB + 32768 + 73728))
    wp_buf = [Buf("wp0"), Buf("wp1")]
    wp_next = [0]

    P = [es.enter_context(nc.psum_tensor("P%d" % i, [128, 1024], F32)) for i in range(4)]
    pbuf = [[Buf("p%d_%d" % (i, h)) for h in range(2)] for i in range(4)]
    bank_rr = [0]

    def next_bank():
        k = bank_rr[0]
        bank_rr[0] = (k + 1) % 8
        i, h = k // 2, k % 2
        return P[i][:, h * 512:(h + 1) * 512], pbuf[i][h]

    dbl_rr = [0]

    def next_dbl():
        i = dbl_rr[0]
        dbl_rr[0] = (i + 1) % 4
        return P[i][:, :], pbuf[i]

    def mm(out, lhsT, rhs, start, stop, reads, writes):
        S.op("pe", lambda e: e.matmul(out, lhsT=lhsT, rhs=rhs, start=start, stop=stop), reads, writes)

    def acc(out, obufs, terms):
        n = len(terms)
        for i, (l, r, rb) in enumerate(terms):
            mm(out, l, r, i == 0, i == n - 1, rb, obufs)

    def tr(out, in_, reads, writes):
        S.op("pe", lambda e: e.transpose(out, in_, ident), list(reads) + [b_const], writes)

    def wload(src_ap, shape, name="w", l=0):
        i = wp_next[0]
        wp_next[0] = 1 - i
        WP[i].reset()
        ap, _ = WP[i].alloc(name, shape, BF16)
        S.dma("sp", ap, src_ap, [b_wqd[("in", l)]], [wp_buf[i]])
        return ap, wp_buf[i]

    def kview(w2d):
        return w2d.rearrange("(kc p) n -> p kc n", p=128)

    b_const = Buf("const")
    b_wq = Buf("wq")
    ident, _ = PERM.alloc("ident", [128], BF16)
    trif, _ = PERM.alloc("trif", [128], F32)
    trib, _ = PERM.alloc("trib", [128], F32)
    ones, _ = PERM.alloc("ones", [128], F32)
    sel3, _ = PERM.alloc("sel3", [16, 128], BF16, parts=96)
    identF, _ = PERM.alloc("identF", [128], F32)
    negm, _ = PERM.alloc("negm", [2, 128], BF16)
    cs, _ = PERM.alloc("cs", [256], BF16)
    wsT, _ = PERM.alloc("wsT", [NL, 4, 128], BF16)
    bs_bc, _ = PERM.alloc("bs_bc", [NL, 512], F32)
    dsk_bc, _ = PERM.alloc("dsk_bc", [NL, 16], F32)
    dtb_bc, _ = PERM.alloc("dtb_bc", [NL, 32], F32)
    a_bc, _ = PERM.alloc("a_bc", [NL, 32], F32)
    fw_bc, _ = PERM.alloc("fw_bc", [D], F32)
    modT, b_modT = PERM.alloc("modT", [NL, 48, NB1], F32)
    s1T, _ = PERM.alloc("s1T", [NL, 8, NB1], F32)
    s2T, _ = PERM.alloc("s2T", [NL, 8, NB1], F32)
    n1T, _ = PERM.alloc("n1T", [NL, 8], F32)
    n2T, _ = PERM.alloc("n2T", [NL, 8], F32)
    bmodT, _ = PERM.alloc("bmodT", [NL, 48], F32)
    cwT, _ = PERM.alloc("cwT", [NL, 12, 5], F32)
    cbT, _ = PERM.alloc("cbT", [NL, 12], F32)
    fcwT, _ = PERM.alloc("fcwT", [NL, 44, 3], F32)
    fcbT, _ = PERM.alloc("fcbT", [NL, 44], F32)
    snT, _ = PERM.alloc("snT", [NL, 8], F32)
    sc, _ = PERM.alloc("sc", [8, NB1], F32)
    ss, b_ss = PERM.alloc("ss", [4], F32)
    ss2, b_ss2 = PERM.alloc("ss2", [4], F32)
    rstd, b_rstd = PERM.alloc("rstd", [4], F32)
    b_ssk = [Buf("ss%d" % k) for k in range(4)]
    b_ss2k = [Buf("ss2_%d" % k) for k in range(4)]
    b_rstdk = [Buf("rstd%d" % k) for k in range(4)]
    eps_c, _ = PERM.alloc("eps_c", [1], F32)
    hT, _ = HT.alloc("hT", [8, TOK], BF16)
    b_hT = [Buf("hT%d" % t) for t in range(NT)]

    def ld(dst, src, **kw):
        S.dma("sp", dst, src, [], [b_const], append=True, **kw)

    ld(ident, ident_in)
    ld(trif, trif_in)
    ld(trib, trib_in)
    ld(ones, ones_in)
    ld(sel3, sel3_in)
    ld(identF, identF_in)
    ld(negm, negm_in)
    ld(cs, cs_in)
    ld(n1T, n1T_in)
    ld(n2T, n2T_in)
    ld(bmodT, bmodT_in)
    ld(cwT, cwT_in)
    ld(cbT, cbT_in)
    ld(fcwT, fcwT_in)
    ld(fcbT, fcbT_in)
    ld(snT, snT_in)
    ld(sc, ccT_in)
    for l in range(NL):
        ld(bs_bc[:, l, :], bs_in[l].partition_broadcast(128))
        ld(dsk_bc[:, l, :], dsk_in[l].partition_broadcast(128))
        ld(dtb_bc[:, l, :], dtb_in[l].partition_broadcast(128))
        ld(a_bc[:, l, :], alog_in[l].partition_broadcast(128))
        S.dma("pool", wsT[:, l], wsT_in[l], [], [b_const], append=True)
    ld(fw_bc, fn_in.partition_broadcast(128))
    S.op("pool", lambda e: e.memset(eps_c, EPS), [], [b_const])
    S.op("act", lambda e: e.activation(out=a_bc, in_=a_bc, func=AF.Exp), [b_const], [b_const])
    S.op("dve", lambda e: e.tensor_scalar(out=a_bc, in0=a_bc, scalar1=-1.0, scalar2=None, op0=ALU.mult), [b_const], [b_const])
    S.op("act", lambda e: e.activation(out=sc, in_=sc, func=AF.Silu), [b_const], [b_const])

    b_wqd = {}

    def conv(key, l, dst2d, kc, w, src3d, off=0):
        bq = b_wqd.setdefault((key, l), Buf("wq_%s%d" % (key, l)))
        S.dma("pool", dst2d[:, off:off + kc * w].rearrange("p (k w) -> p k w", k=kc), src3d, [], [bq], append=True, max_dma_last_dim=4096)

    for l in range(NL):
        wi = kview(w_in[l])
        for cb in range(3):
            conv("in", l, wq_x[l, cb], 8, 512, wi[:, :, cb * 512:(cb + 1) * 512])
        conv("in", l, wq_dt[l, 0], 8, 32, wi[:, :, OFF_DT:OFF_DT + 32])
        conv("in", l, wq_z[l, 0], 8, 1024, wi[:, :, OFF_Z:OFF_Z + 1024])
        conv("in", l, wq_f[l, 0], 8, 512, wi[:, :, OFF_FFT:OFF_FFT + 512])
        conv("in", l, wq_uv[l, 0], 8, 1024, wi[:, :, OFF_GMLP:OFF_GMLP + 1024])
        for jb in range(4):
            for br in range(3):
                conv("in", l, wq_g[l, jb], 8, 256, wi[:, :, OFF_GATE + br * D + jb * 256:OFF_GATE + br * D + (jb + 1) * 256], off=br * 2048)
            conv("mo", l, wq_mo[l, jb], 8, 256, kview(w_sso[l])[:, :, jb * 256:(jb + 1) * 256], off=0)
            conv("mo", l, wq_mo[l, jb], 4, 256, kview(w_ffo[l])[:, :, jb * 256:(jb + 1) * 256], off=2048)
            conv("mo", l, wq_mo[l, jb], 4, 256, kview(w_gmo[l])[:, :, jb * 256:(jb + 1) * 256], off=3072)
        for hf in range(2):
            conv("out", l, wq_out[l, hf], 8, 512, kview(w_out[l])[:, :, hf * 512:(hf + 1) * 512])
        for pb6 in range(6):
            ncol = (4 if pb6 < 5 else 2) * 128
            conv("up", l, wq_ua[l, pb6], 8, ncol, kview(w_up[l])[:, :, pb6 * 512:pb6 * 512 + ncol])
            conv("up", l, wq_uvv[l, pb6], 8, ncol, kview(w_up[l])[:, :, DFF + pb6 * 512:DFF + pb6 * 512 + ncol])
        for qt in range(4):
            conv("dn", l, wq_dn[l, qt], 22, 256, kview(w_dn[l])[:, :, qt * 256:(qt + 1) * 256])

    R3.reset()
    wm = [R3.alloc("wm%d" % i, [8, 512], F32) for i in range(2)]
    mrows = [R3.alloc("mrow%d" % i, [512], F32, parts=NB1) for i in range(2)]
    bmrs = [R3.alloc("bmr%d" % i, [512], F32, parts=NB1) for i in range(2)]
    b_modrow = Buf("modrow")
    for l in range(NL):
        for blk in range(12):
            wt, wb = wm[blk % 2]
            S.dma("sp", wt, kview(w_mod[l])[:, :, blk * 512:(blk + 1) * 512], [], [wb])
            bmr, b_bmr = bmrs[blk % 2]
            S.dma("sp", bmr, bmod_row[l, blk * 512:(blk + 1) * 512].partition_broadcast(NB1), [], [b_bmr])
            mrow, b_mrow = mrows[blk % 2]
            pa, pb_ = next_bank()
            acc(pa[0:NB1, :], [pb_], [(sc[:, kc, :], wt[:, kc, :], [wb, b_const]) for kc in range(8)])
            S.op("dve", lambda e: e.tensor_tensor(out=mrow, in0=pa[0:NB1, :], in1=bmr, op=ALU.add),
                 [pb_, b_bmr], [b_mrow])
            S.dma("pool", modrow_d[l, :, blk * 512:(blk + 1) * 512], mrow, [b_mrow], [b_modrow], append=True)
            pa, pb_ = next_bank()
            for j in range(4):
                mm(pa[:, j * 8:j * 8 + NB1], mrow[0:NB1, j * 128:(j + 1) * 128], identF[0:NB1, 0:NB1], True, True, [b_mrow, b_const], [pb_])
            S.op("dve", lambda e: e.tensor_copy(out=modT[:, l, blk * 4:(blk + 1) * 4, :], in_=pa[:, 0:32].rearrange("p (j c) -> p j c", j=4)[:, :, 0:NB1]),
                 [pb_], [b_modT])
    for l in range(NL):
        S.op("dve", lambda e: e.scalar_tensor_tensor(out=s1T[:, l], in0=modT[:, l, 8:16, :], scalar=1.0,
                                                      in1=n1T[:, l, :].unsqueeze(2).broadcast_to([128, 8, NB1]), op0=ALU.add, op1=ALU.mult),
             [b_modT, b_const], [b_modT])
        S.op("dve", lambda e: e.scalar_tensor_tensor(out=s2T[:, l], in0=modT[:, l, 32:40, :], scalar=1.0,
                                                      in1=n2T[:, l, :].unsqueeze(2).broadcast_to([128, 8, NB1]), op0=ALU.add, op1=ALU.mult),
             [b_modT, b_const], [b_modT])
    S.barrier()

    def tcol(t):
        return t * 128

    def acol(t):
        return t * 128 + (4 if t >= 2 else 0)

    GROUPS = [(0, 2)] + [(2 + 4 * i, 4) for i in range(4)]

    def rstd_from_ss(k, n):
        S.op("act", lambda e: e.activation(out=ss2[:, k:k + 1], in_=ss[:, k:k + 1], func=AF.Ln, scale=1.0 / n, bias=eps_c[:, 0:1]), [b_ssk[k], b_const], [b_ss2k[k]])
        S.op("act", lambda e: e.activation(out=rstd[:, k:k + 1], in_=ss2[:, k:k + 1], func=AF.Exp, scale=-0.5), [b_ss2k[k]], [b_rstdk[k]])

    def norm_ss(xt, xb, xn, b_xn, k=0):
        S.op("dve", lambda e: e.scalar_tensor_tensor(out=xn, in0=xt, scalar=1.0, in1=xt, op0=ALU.mult, op1=ALU.mult, accum_out=ss[:, k:k + 1]), xb, [b_xn, b_ssk[k]], nsl=2)
        rstd_from_ss(k, D)

    def norm_xn(xt, xb, xn, b_xn, k=0):
        S.op("dve", lambda e: e.tensor_scalar(out=xn, in0=xt, scalar1=rstd[:, k:k + 1], scalar2=None, op0=ALU.mult), list(xb) + [b_rstdk[k]], [b_xn])

    def norm_front(xt, xb, xn, b_xn, k=0):
        norm_ss(xt, xb, xn, b_xn, k)
        norm_xn(xt, xb, xn, b_xn, k)

    def norm_back(xn, b_xn, l, bi, which, dstT, dst_bufs, dcol):
        sT = s1T if which == 1 else s2T
        boff = 0 if which == 1 else 24
        pa, pbs = next_dbl()
        pT = pa[:, 0:512].bitcast(BF16)
        for dc in range(8):
            tr(pT[:, dc * 128:(dc + 1) * 128], xn[:, dc * 128:(dc + 1) * 128], [b_xn], [pbs[0]])
        for dc in range(8):
            eng = "act" if dc % 2 == 0 else "dve"
            o = dstT[:, dc, dcol:dcol + 128]
            i_ = pT[:, dc * 128:(dc + 1) * 128]
            sA = sT[:, l, dc, bi:bi + 1]
            bA = modT[:, l, boff + dc, bi:bi + 1]
            if eng == "act":
                S.op("act", lambda e: e.activation(out=o, in_=i_, func=AF.Identity, scale=sA, bias=bA), [pbs[0], b_modT], dst_bufs)
            else:
                S.op("dve", lambda e: e.tensor_scalar(out=o, in0=i_, scalar1=sA, scalar2=bA, op0=ALU.mult, op1=ALU.add), [pbs[0], b_modT], dst_bufs)

    def norm_to_hT(xt, xb, l, bi, which, t, junk, b_junk, xn, b_xn, dstT=None, dst_bufs=None, dcol=None):
        norm_front(xt, xb, xn, b_xn)
        norm_back(xn, b_xn, l, bi, which, dstT, dst_bufs, dcol)

    def xrow(b, t):
        if t < 2:
            return ctx_in[b, t * 128:(t + 1) * 128, :]
        return x_in[b, (t - 2) * 128:(t - 1) * 128, :]

    def phase1(b):
        S.mark("phase1 b%d" % b)
        R4.reset()
        xl = [R4.alloc("xl%d" % i, [D], F32) for i in range(3)]
        xns = [R4.alloc("xn%d" % i, [D], BF16) for i in range(2)]
        for t in range(NT + 1):
            if t < NT:
                xt, xb = xl[t % 3]
                S.dma("sp", xt, xrow(b, t), [], [xb])
                norm_front(xt, [xb], *xns[t % 2])
            if t >= 1:
                tp_ = t - 1
                bi = NB if tp_ < 2 else b
                norm_back(xns[tp_ % 2][0], xns[tp_ % 2][1], 0, bi, 1, hT, [b_hT[tp_]], tcol(tp_))
        S.barrier()

    def phaseI(b, l):
        tiles_y = list(range(NT)) if l == 0 else list(range(2, NT))
        S.mark("P2 b%d l%d" % (b, l))
        R3.reset()
        R4.reset()
        xs_tok, b_xs = R3.alloc("xs_tok", [NT, D], BF16)
        B_tok, b_Bt = R3.alloc("B_tok", [NT, 256], BF16)
        BT, b_BT = R3.alloc("BT", [2, 2312], BF16)
        CT, b_CT = R3.alloc("CT", [2, 2312], BF16)
        raws = [R4.alloc("raw%d" % i, [2312], BF16) for i in range(2)]
        caccs = [R4.alloc("cacc%d" % i, [2308], F32) for i in range(2)]
        actc, b_actc = R4.alloc("actc", [2312], BF16)
        for i in range(2):
            S.op("pool", lambda e: e.memset(raws[i][0], 0.0), [], [raws[i][1]])
        def p2_trans(c, dst, dbuf):
            for t0 in range(0, NT, 6):
                pa, pbs = next_dbl()
                pT = pa.bitcast(BF16)
                for i in range(6):
                    t = t0 + i
                    tr(pT[:, i * 128:(i + 1) * 128], dst[:, acol(t):acol(t) + 128], [dbuf], [pbs[0]])
                if c < 8:
                    o = xs_tok[:, t0:t0 + 6, c * 128:(c + 1) * 128]
                    ob = b_xs
                else:
                    o = B_tok[:, t0:t0 + 6, (c - 8) * 128:(c - 7) * 128]
                    ob = b_Bt
                S.op("act", lambda e: e.activation(out=o, in_=pT[:, 0:768].rearrange("p (a b) -> p a b", a=6), func=AF.Identity), [pbs[0]], [ob])

        wts = {}

        def p2_A(c):
            cb, j = c // 4, c % 4
            if j == 0:
                wts[cb] = wload(wq_x[l, cb].rearrange("p (k w) -> p k w", k=8), [8, 512], "w", l=l)
            wt, wb = wts[cb]
            raw, b_raw = raws[c % 2]
            cacc, b_cacc = caccs[c % 2]
            for (t0, ntl) in GROUPS:
                n = ntl * 128
                c0 = tcol(t0)
                pa, pb_ = next_bank()
                acc(pa[:, 0:n], [pb_], [(wt[:, kc, j * 128:(j + 1) * 128], hT[:, kc, c0:c0 + n], [wb] + b_hT[t0:t0 + ntl]) for kc in range(8)])
                r0 = c0 + 2 if t0 == 0 else c0 + 6
                S.op("act", lambda e: e.activation(out=raw[:, r0:r0 + n], in_=pa[:, 0:n], func=AF.Identity), [pb_], [b_raw])
            S.op("act", lambda e: e.activation(out=cacc, in_=raw[:, 0:2308], func=AF.Identity, scale=cwT[:, l, c, 0:1]),
                 [b_raw, b_const], [b_cacc])

        def p2_B(c):
            raw, b_raw = raws[c % 2]
            cacc, b_cacc = caccs[c % 2]
            for k in range(1, 5):
                S.op("dve", lambda e: e.scalar_tensor_tensor(out=cacc, in0=raw[:, k:k + 2308], scalar=cwT[:, l, c, k:k + 1], in1=cacc,
                                                              op0=ALU.mult, op1=ALU.add), [b_raw, b_const, b_cacc], [b_cacc])

        def p2_C(c):
            cacc, b_cacc = caccs[c % 2]
            if c < 8:
                dst, dbuf = actc[:, 0:2308], b_actc
            elif c < 10:
                dst, dbuf = BT[:, c - 8, 0:2308], b_BT
            else:
                dst, dbuf = CT[:, c - 10, 0:2308], b_CT
            S.op("act", lambda e: e.activation(out=dst, in_=cacc, func=AF.Silu, bias=cbT[:, l, c:c + 1]), [b_cacc, b_const], [dbuf])
            return (c, dst, dbuf) if c < 10 else None

        p2_A(0)
        p2_B(0)
        for c in range(12):
            if c + 1 < 12:
                p2_A(c + 1)
            pend_tr = p2_C(c)
            if c + 1 < 12:
                p2_B(c + 1)
            if pend_tr is not None:
                p2_trans(*pend_tr)
        S.barrier()
        S.mark("P3 b%d l%d" % (b, l))
        R4.reset()
        la_full, b_la = R4.alloc("la", [NT + 1, 32], F32)
        bE_full, b_bE = R4.alloc("biasE", [NT + 1, 32], F32)
        la = la_full[:, 0:NT, :]
        biasE = bE_full[:, 0:NT, :]
        S.op("pool", lambda e: e.memset(la_full[:, NT, :], 0.0), [], [b_la])
        S.op("pool", lambda e: e.memset(bE_full[:, NT, :], 0.0), [], [b_bE])
        dq, b_dq = R4.alloc("dq", [NT, 32], F32)
        r4_p5 = R4.off
        dt_, b_dt = R4.alloc("dt", [NT, 32], F32)
        cum, b_cum = R4.alloc("cum", [NT, 32], F32)
        tot, b_tot = R4.alloc("tot", [NT, 32], F32)
        ww, b_ww = R4.alloc("ww", [NT, 32], F32)
        wdt_ap, wdt_b = wload(wq_dt[l, 0].rearrange("p (k w) -> p k w", k=8), [8, 32], "wdt", l=l)
        pa, pbs = next_dbl()
        for t in range(NT):
            acc(pa[:, t * 32:(t + 1) * 32], [pbs[t // 16]], [(hT[:, kc, tcol(t):tcol(t) + 128], wdt_ap[:, kc, :], [wdt_b, b_hT[t]]) for kc in range(8)])
        S.op("dve", lambda e: e.tensor_tensor(out=dt_, in0=pa[:, 0:NT * 32].rearrange("p (a b) -> p a b", a=NT),
                                               in1=dtb_bc[:, l, :].unsqueeze(1).broadcast_to([128, NT, 32]), op=ALU.add), [pbs[0], pbs[1], b_const], [b_dt])
        S.op("act", lambda e: e.activation(out=dt_, in_=dt_, func=AF.Exp), [b_dt], [b_dt])
        S.op("act", lambda e: e.activation(out=dt_, in_=dt_, func=AF.Ln, bias=1.0), [b_dt], [b_dt])
        S.op("dve", lambda e: e.tensor_tensor(out=la, in0=dt_, in1=a_bc[:, l, :].unsqueeze(1).broadcast_to([128, NT, 32]), op=ALU.mult), [b_dt, b_const], [b_la])
        S.op("act", lambda e: e.activation(out=biasE, in_=dt_, func=AF.Ln), [b_dt], [b_bE])
        pa, pbs = next_dbl()
        pa2, pbs2 = next_dbl()
        for t in range(NT):
            mm(pa[:, t * 32:t * 32 + 16], trif, la[:, t, 0:16], True, True, [b_la, b_const], [pbs[t // 16]])
            mm(pa[:, t * 32 + 16:t * 32 + 32], trib, la[:, t, 16:32], True, True, [b_la, b_const], [pbs[t // 16]])
            mm(pa2[:, t * 32:(t + 1) * 32], ones, la[:, t, :], True, True, [b_la, b_const], [pbs2[t // 16]])
        S.op("act", lambda e: e.activation(out=cum, in_=pa[:, 0:NT * 32].rearrange("p (a b) -> p a b", a=NT), func=AF.Identity), [pbs[0], pbs[1]], [b_cum])
        S.op("act", lambda e: e.activation(out=tot, in_=pa2[:, 0:NT * 32].rearrange("p (a b) -> p a b", a=NT), func=AF.Identity), [pbs2[0], pbs2[1]], [b_tot])
        S.op("dve", lambda e: e.tensor_tensor(out=biasE, in0=biasE, in1=cum, op=ALU.subtract), [b_bE, b_cum], [b_bE])
        S.op("act", lambda e: e.activation(out=dq, in_=cum, func=AF.Exp), [b_cum], [b_dq])
        S.op("dve", lambda e: e.tensor_tensor(out=ww, in0=tot, in1=cum, op=ALU.subtract), [b_tot, b_cum], [b_ww])
        S.op("act", lambda e: e.activation(out=ww, in_=ww, func=AF.Exp), [b_ww], [b_ww])
        S.op("dve", lambda e: e.tensor_tensor(out=ww, in0=ww, in1=dt_, op=ALU.mult), [b_ww, b_dt], [b_ww])
        S.op("act", lambda e: e.activation(out=tot, in_=tot, func=AF.Exp), [b_tot], [b_tot])
        S.mark("P4 b%d l%d" % (b, l))
        Sts = [R4.alloc("St%d" % i, [D], F32) for i in range(2)]
        tmpS, b_tmpS = R4.alloc("tmpS", [D], F32)
        xw, b_xw = R4.alloc("xw", [D], BF16)
        sst = [R4.alloc("sst%d" % i, [D], BF16) for i in range(2)]
        b_sstd = [[Buf("sstd%d_%d" % (d, t)) for t in range(NT)] for d in range(2)]
        groups = GROUPS if l == 0 else GROUPS[1:]
        uT, b_uT = R4.alloc("uT", [4, 512], BF16)
        vs = [R4.alloc("vs%d" % i, [512], BF16) for i in range(1)] * 2
        tmpg, b_tmpg = R4.alloc("tmpg", [512], F32)
        ygo = [R4.alloc("ygo%d" % i, [4, 128], BF16) for i in range(2)]
        wuv, wuvb = wload(wq_uv[l, 0].rearrange("p (k w) -> p k w", k=8), [8, 1024], "wuv", l=l)

        def gen_P4():
            k_sst = 0
            for d in range(2):
                order = list(range(NT)) if d == 0 else [1, 0] + list(range(NT - 1, 1, -1))
                cur = 0
                S.op("pool", lambda e: e.memset(Sts[0][0], 0.0), [], [Sts[0][1]])
                for t in order:
                    St, b_St = Sts[cur]
                    if t in tiles_y:
                        sa, sb = sst[k_sst % 2]
                        k_sst += 1
                        S.op("act", lambda e: e.activation(out=sa, in_=St, func=AF.Identity), [b_St], [sb])
                        S.dma("sp", sst_d[b, d, t], sa, [sb], [b_sstd[d][t]])
                    if t == order[-1]:
                        break
                    wv = ww[:, t, d * 16:(d + 1) * 16].unsqueeze(2).broadcast_to([128, 16, 64])
                    S.op("pool", lambda e: e.tensor_tensor(out=xw.rearrange("p (h c) -> p h c", h=16),
                                                            in0=xs_tok[:, t, :].rearrange("p (h c) -> p h c", h=16), in1=wv, op=ALU.mult),
                         [b_xs, b_ww], [b_xw])
                    pa, pbs = next_dbl()
                    for g in range(2):
                        mm(pa[:, g * 512:(g + 1) * 512], B_tok[:, t, g * 128:(g + 1) * 128], xw[:, g * 512:(g + 1) * 512], True, True, [b_Bt, b_xw], [pbs[g]])
                    dav = tot[:, t, d * 16:(d + 1) * 16].unsqueeze(2).broadcast_to([128, 16, 64])
                    S.op("dve", lambda e: e.tensor_tensor(out=tmpS.rearrange("p (h c) -> p h c", h=16), in0=St.rearrange("p (h c) -> p h c", h=16),
                                                           in1=dav, op=ALU.mult), [b_St, b_tot], [b_tmpS])
                    Sn, b_Sn = Sts[1 - cur]
                    S.op("dve", lambda e: e.tensor_tensor(out=Sn, in0=pa, in1=tmpS, op=ALU.add), [pbs[0], pbs[1], b_tmpS], [b_Sn])
                    cur = 1 - cur
                    yield

        def gen_P7():
            k_v = 0
            for (t0, ntl) in groups:
                n = ntl * 128
                c0 = tcol(t0)
                for g in range(4):
                    pa, pb_ = next_bank()
                    acc(pa[:, 0:n], [pb_], [(wuv[:, kc, g * 128:(g + 1) * 128], hT[:, kc, c0:c0 + n], [wuvb] + b_hT[t0:t0 + ntl]) for kc in range(8)])
                    S.op("act", lambda e: e.activation(out=uT[:, g, 0:n], in_=pa[:, 0:n], func=AF.Gelu_apprx_tanh), [pb_], [b_uT])
                    if g % 2 == 1:
                        yield
                for i in range(ntl):
                    t = t0 + i
                    pa, pb_ = next_bank()
                    acc(pa, [pb_], [(hT[:, kc, tcol(t):tcol(t) + 128], wuv[:, kc, 512:1024], [wuvb, b_hT[t]]) for kc in range(8)])
                    va, vb = vs[k_v % 2]
                    S.op("act", lambda e: e.activation(out=va, in_=pa, func=AF.Gelu_apprx_tanh), [pb_], [vb])
                    pa2, pb2 = next_bank()
                    for g in range(4):
                        mm(pa2[:, g * 128:(g + 1) * 128], va[:, g * 128:(g + 1) * 128], wsT[:, l, g, :], True, True, [vb, b_const], [pb2])
                    S.op("dve", lambda e: e.tensor_tensor(out=tmpg, in0=pa2, in1=bs_bc[:, l, :], op=ALU.add), [pb2, b_const], [b_tmpg])
                    yo, yob = ygo[k_v % 2]
                    k_v += 1
                    S.op("dve", lambda e: e.tensor_tensor(out=yo, in0=tmpg.rearrange("p (g q) -> p g q", g=4), in1=uT[:, :, i * 128:(i + 1) * 128], op=ALU.mult),
                         [b_tmpg, b_uT], [yob])
                    S.dma("sp", yg_d[b, t].rearrange("p (c n) -> p c n", c=4), yo, [yob], [b_ygd], append=True)
                    yield

        S.mark("P7 b%d l%d" % (b, l))
        g4, g7 = gen_P4(), gen_P7()
        d4 = d7 = False
        while not (d4 and d7):
            if not d4:
                try:
                    next(g4)
                except StopIteration:
                    d4 = True
            if not d7:
                try:
                    next(g7)
                except StopIteration:
                    d7 = True
        S.barrier()
        S.mark("P5 b%d l%d" % (b, l))
        R4.off = r4_p5
        wz, wzb = wload(wq_z[l, 0].rearrange("p (k w) -> p k w", k=8), [8, D], "wz", l=l)
        Sp = [R3.alloc("Sp%d" % d, [D], BF16) for d in range(2)]
        zs, b_zs = R3.alloc("zs", [D], BF16)
        CBm = [R4.alloc("CBm%d" % d, [2, 128], BF16) for d in range(2)]
        C3s = [R4.alloc("C3_%d" % d, [256], BF16, parts=96) for d in range(2)]
        r1, b_r1 = R3.alloc("r1", [256], F32, parts=96)
        m2, b_m2 = R3.alloc("m2", [256], BF16, parts=96)
        r2, b_r2 = R3.alloc("r2", [256], F32, parts=96)
        la_flat = la_full.rearrange("p a b -> p (a b)")
        bE_flat = bE_full.rearrange("p a b -> p (a b)")
        rep3 = [[R4.alloc("rep3_%d_%d" % (d, w_), [96], F32) for w_ in range(2)] for d in range(2)]
        Es = [R4.alloc("E%d" % d, [16, 128], BF16) for d in range(2)]
        M = [R4.alloc("M%d" % d, [16, 128], BF16) for d in range(2)]
        Mb = [[Buf("M%d_%d" % (d, g)) for g in range(2)] for d in range(2)]
        tmpAs = [R4.alloc("tmpA%d" % d, [D], BF16) for d in range(2)]
        yvs = [R4.alloc("yv%d" % i, [D], F32) for i in range(2)]
        yns = [R4.alloc("yn%d" % i, [D], BF16) for i in range(1)] * 2
        ysT = [R4.alloc("ysT%d" % i, [8, 128], BF16) for i in range(1)] * 2
        tris = (trif, trib)

        def p5_front(it, t):
            ac = acol(t)
            for d in range(2):
                S.dma("sp", Sp[d][0], sst_d[b, d, t], [b_sstd[d][t]], [Sp[d][1]])
            for d in range(2):
                pq, pqb = next_bank()
                off = t * 32 + d * 16
                lap, b_lap = rep3[d][0]
                bep, b_bep = rep3[d][1]
                S.op("pool", lambda e: e.tensor_copy(out=lap.rearrange("p (r c) -> p r c", r=3), in_=la_flat[:, off:off + 32].unsqueeze(1).broadcast_to([128, 3, 32])),
                     [b_la], [b_lap])
                S.op("pool", lambda e: e.tensor_copy(out=bep.rearrange("p (r c) -> p r c", r=3), in_=bE_flat[:, off:off + 32].unsqueeze(1).broadcast_to([128, 3, 32])),
                     [b_bE], [b_bep])
                mm(pq[0:96, 0:128], lap, tris[d], True, True, [b_lap, b_const], [pqb])
                mm(pq[0:96, 128:256], bep, identF, True, True, [b_bep, b_const], [pqb])
                C3, b_C3 = C3s[d]
                S.op("act", lambda e: e.activation(out=C3, in_=pq[0:96, 0:256], func=AF.Identity), [pqb], [b_C3])
                S.op("dve", lambda e: e.tensor_tensor(out=r1[0:96, :], in0=pq[0:96, 0:256], in1=C3[0:96, :], op=ALU.subtract), [pqb, b_C3], [b_r1])
                S.op("act", lambda e: e.activation(out=C3[32:64, :], in_=r1[32:64, :], func=AF.Identity), [b_r1], [b_C3])
                S.op("act", lambda e: e.activation(out=m2[64:96, :], in_=r1[64:96, :], func=AF.Identity), [b_r1], [b_m2])
                S.op("dve", lambda e: e.tensor_tensor(out=r2[64:96, :], in0=r1[64:96, :], in1=m2[64:96, :], op=ALU.subtract), [b_r1, b_m2], [b_r2])
                S.op("act", lambda e: e.activation(out=C3[64:96, :], in_=r2[64:96, :], func=AF.Identity), [b_r2], [b_C3])
            pc, pcb = next_bank()
            for g in range(2):
                mm(pc[:, g * 128:(g + 1) * 128], BT[:, g, ac:ac + 128], CT[:, g, ac:ac + 128], True, True, [b_BT, b_CT], [pcb])
            S.op("act", lambda e: e.activation(out=CBm[0][0], in_=pc[:, 0:256].rearrange("p (g q) -> p g q", g=2), func=AF.Identity), [pcb], [CBm[0][1]])
            yield
            for d in range(2):
                po, pob = next_dbl()
                for g in range(2):
                    mm(po[:, g * 512:(g + 1) * 512], CT[:, g, ac:ac + 128], Sp[d][0][:, g * 512:(g + 1) * 512], True, True, [b_CT, Sp[d][1]], [pob[g]])
                dqv = dq[:, t, d * 16:(d + 1) * 16].unsqueeze(2).broadcast_to([128, 16, 64])
                S.op("dve", lambda e: e.tensor_tensor(out=tmpAs[d][0].rearrange("p (h c) -> p h c", h=16), in0=po.rearrange("p (h c) -> p h c", h=16),
                                                       in1=dqv, op=ALU.mult), pob + [b_dq], [tmpAs[d][1]])
            yield
            for d in range(2):
                E, b_E = Es[d]
                C3, b_C3 = C3s[d]
                for h4 in range(4):
                    pe_, peb = next_bank()
                    mm(pe_[:, 0:512], C3[:, 128:256], sel3[:, h4 * 4:(h4 + 1) * 4, :], True, False, [b_C3, b_const], [peb])
                    mm(pe_[:, 0:512], ident, negm[:, d, :].unsqueeze(1).broadcast_to([128, 4, 128]), False, False, [b_const], [peb])
                    for hh in range(4):
                        h = h4 * 4 + hh
                        mm(pe_[:, hh * 128:(hh + 1) * 128], sel3[:, h, :], C3[:, 0:128], False, hh == 3, [b_C3, b_const], [peb])
                    S.op("act", lambda e: e.activation(out=E[:, h4 * 4:(h4 + 1) * 4, :], in_=pe_[:, 0:512].rearrange("p (h q) -> p h q", h=4), func=AF.Exp),
                         [peb], [b_E])
                for g in range(2):
                    S.op("dve",
                         lambda e: e.tensor_tensor(out=M[d][0][:, g * 8:(g + 1) * 8, :], in0=E[:, g * 8:(g + 1) * 8, :],
                                                   in1=CBm[0][0][:, g, :].unsqueeze(1).broadcast_to([128, 8, 128]), op=ALU.mult),
                         [b_E, CBm[0][1]], [Mb[d][g]])
                yield
            pyd, pydb = next_dbl()
            for h in range(16):
                o = pyd[:, h * 64:(h + 1) * 64]
                mm(o, M[0][0][:, h, :], xs_tok[:, t, h * 64:(h + 1) * 64], True, False, [Mb[0][h // 8], b_xs], [pydb[h // 8]])
                mm(o, M[1][0][:, h, :], xs_tok[:, t, h * 64:(h + 1) * 64], False, True, [Mb[1][h // 8], b_xs], [pydb[h // 8]])
            yv, b_yv = yvs[it % 2]
            S.op("dve", lambda e: e.tensor_tensor(out=yv, in0=pyd, in1=tmpAs[0][0], op=ALU.add), pydb + [tmpAs[0][1]], [b_yv])
            S.op("dve", lambda e: e.tensor_tensor(out=yv, in0=yv, in1=tmpAs[1][0], op=ALU.add), [b_yv, tmpAs[1][1]], [b_yv])

        def p5_mid(it, t):
            yv, b_yv = yvs[it % 2]
            yn, b_yn = yns[0]
            pz, pzb = next_dbl()
            for hf in range(2):
                acc(pz[:, hf * 512:(hf + 1) * 512], [pzb[hf]], [(hT[:, kc, tcol(t):tcol(t) + 128], wz[:, kc, hf * 512:(hf + 1) * 512], [wzb, b_hT[t]]) for kc in range(8)])
            S.op("act", lambda e: e.activation(out=zs, in_=pz, func=AF.Silu), pzb, [b_zs])
            yield
            dsv = dsk_bc[:, l, :].unsqueeze(2).broadcast_to([128, 16, 64])
            S.op("pool", lambda e: e.tensor_tensor(out=yn.rearrange("p (h c) -> p h c", h=16), in0=xs_tok[:, t, :].rearrange("p (h c) -> p h c", h=16),
                                                    in1=dsv, op=ALU.mult), [b_xs, b_const], [b_yn])
            S.op("pool", lambda e: e.tensor_tensor(out=yv, in0=yv, in1=yn, op=ALU.add), [b_yv, b_yn], [b_yv])
            S.op("dve", lambda e: e.tensor_tensor(out=yv, in0=yv, in1=zs, op=ALU.mult), [b_yv, b_zs], [b_yv])
            yield
            for g in range(2):
                S.op("act", lambda e: e.activation(out=yn[:, g * 512:(g + 1) * 512], in_=yv[:, g * 512:(g + 1) * 512], func=AF.Square, accum_out=ss[:, g:g + 1]), [b_yv], [b_yn, b_ssk[g]])
            for g in range(2):
                rstd_from_ss(g, 512)
            for g in range(2):
                S.op("dve", lambda e: e.tensor_scalar(out=yn[:, g * 512:(g + 1) * 512], in0=yv[:, g * 512:(g + 1) * 512], scalar1=rstd[:, g:g + 1],
                                                       scalar2=None, op0=ALU.mult), [b_yv, b_rstdk[g]], [b_yn])

        def p5_back(it, t):
            yn, b_yn = yns[it % 2]
            pa, pbs = next_dbl()
            pT = pa[:, 0:512].bitcast(BF16)
            for dc in range(8):
                tr(pT[:, dc * 128:(dc + 1) * 128], yn[:, dc * 128:(dc + 1) * 128], [b_yn], [pbs[0]])
            yo, yob = ysT[it % 2]
            for dc in range(8):
                if dc % 2 == 0:
                    S.op("act", lambda e: e.activation(out=yo[:, dc, :], in_=pT[:, dc * 128:(dc + 1) * 128], func=AF.Identity, scale=snT[:, l, dc:dc + 1]), [pbs[0], b_const], [yob])
                else:
                    S.op("dve", lambda e: e.tensor_scalar(out=yo[:, dc, :], in0=pT[:, dc * 128:(dc + 1) * 128], scalar1=snT[:, l, dc:dc + 1], scalar2=None, op0=ALU.mult),
                         [pbs[0], b_const], [yob])
            S.dma("act", ys_d[b, t].rearrange("p (c n) -> p c n", c=8), yo, [yob], [b_ysd], append=True)

        def p5_tail(it, t):
            yield from p5_mid(it, t)
            yield
            p5_back(it, t)

        for _ in p5_front(0, tiles_y[0]):
            pass
        for it, t in enumerate(tiles_y):
            fg = p5_front(it + 1, tiles_y[it + 1]) if it + 1 < len(tiles_y) else iter(())
            tg = p5_tail(it, t)
            fdone = tdone = False
            while not (fdone and tdone):
                if not fdone:
                    try:
                        next(fg)
                    except StopIteration:
                        fdone = True
                if not tdone:
                    try:
                        next(tg)
                    except StopIteration:
                        tdone = True
        S.barrier()
        S.mark("P6 b%d l%d" % (b, l))
        R3.reset()
        R4.reset()
        AB, b_AB = R3.alloc("AB", [NT, 4, 256], BF16)
        fT, b_fT = R4.alloc("fT", [4, 512], BF16)
        clb = [R4.alloc("clb%d" % i, [2, 2, 512], BF16) for i in range(3)]
        yfo = [R4.alloc("yfo%d" % i, [4, 4, 128], BF16) for i in range(2)]
        wf, wfb = wload(wq_f[l, 0].rearrange("p (k w) -> p k w", k=8), [8, 512], "wf", l=l)
        groups = GROUPS if l == 0 else GROUPS[1:]
        for (t0, ntl) in groups:
            n = ntl * 128
            c0 = tcol(t0)
            for g in range(4):
                pa, pb_ = next_bank()
                acc(pa[:, 0:n], [pb_], [(wf[:, kc, g * 128:(g + 1) * 128], hT[:, kc, c0:c0 + n], [wfb] + b_hT[t0:t0 + ntl]) for kc in range(8)])
                if g % 2 == 0:
                    S.op("act", lambda e: e.activation(out=fT[:, g, 0:n], in_=pa[:, 0:n], func=AF.Identity), [pb_], [b_fT])
                else:
                    S.op("dve", lambda e: e.tensor_copy(out=fT[:, g, 0:n], in_=pa[:, 0:n]), [pb_], [b_fT])
            for i in range(ntl):
                t = t0 + i
                pa, pbs = next_dbl()
                for g in range(4):
                    mm(pa[:, g * 256:(g + 1) * 256], fT[:, g, i * 128:(i + 1) * 128], cs, True, True, [b_fT, b_const], [pbs[g // 2]])
                if i % 2 == 0:
                    S.op("act", lambda e: e.activation(out=AB[:, t], in_=pa.rearrange("p (g c) -> p g c", g=4), func=AF.Identity), pbs, [b_AB])
                else:
                    S.op("dve", lambda e: e.tensor_copy(out=AB[:, t], in_=pa.rearrange("p (g c) -> p g c", g=4)), pbs, [b_AB])
        k_cl = 0
        k_yf = 0
        segs = [(2, 16, cst_in, 512, 4)]
        if l == 0:
            segs.append((0, 2, cst256_in, 256, 1))
        for (tb, ntt, cst_t, kw, nkb) in segs:
            for kb in range(nkb):
                pas = [next_bank() for g in range(4)]
                for tt2 in range(ntt // 2):
                    ca, cbuf = clb[k_cl % 3]
                    k_cl += 1
                    S.dma("sp", ca[:, :, :, 0:kw], cst_t[kb, tt2].rearrange("p (a c k) -> p a c k", a=2, c=2), [], [cbuf])
                    for a_ in range(2):
                        tt = tt2 * 2 + a_
                        for g in range(4):
                            mm(pas[g][0][:, 0:kw], AB[:, tb + tt, g, 0:128], ca[:, a_, 0, 0:kw], tt == 0, False, [b_AB, cbuf], [pas[g][1]])
                            mm(pas[g][0][:, 0:kw], AB[:, tb + tt, g, 128:256], ca[:, a_, 1, 0:kw], False, tt == ntt - 1, [b_AB, cbuf], [pas[g][1]])
                yo, yob = yfo[k_yf % 2]
                k_yf += 1
                ntile = kw // 128
                for g in range(4):
                    src_ = pas[g][0][:, 0:kw].rearrange("p (t n) -> p t n", t=ntile)
                    if g % 2 == 0:
                        S.op("act", lambda e: e.activation(out=yo[:, 0:ntile, g, :], in_=src_, func=AF.Identity), [pas[g][1]], [yob])
                    else:
                        S.op("dve", lambda e: e.tensor_copy(out=yo[:, 0:ntile, g, :], in_=src_), [pas[g][1]], [yob])
                tq = tb + kb * 4
                S.dma("act", yf_d[b, tq:tq + ntile].rearrange("t p (g n) -> p t g n", g=4), yo[:, 0:ntile], [yob], [b_yfd], append=True)
        S.barrier()

    b_ysd = Buf("ysd")
    b_yfd = Buf("yfd")
    b_ygd = Buf("ygd")
    b_xres = Buf("xres")
    b_out = Buf("out")

    wp4_buf = [Buf("wp4_%d" % i) for i in range(4)]
    wp4_next = [0]
    WP4 = [Region(arena, WPB + i * 12288, 12288) for i in range(4)]

    def wl4(shape, srcs, wkey):
        i = wp4_next[0]
        wp4_next[0] = (i + 1) % 4
        WP4[i].reset()
        ap, _ = WP4[i].alloc("w4", shape, BF16)
        for k, (sl, src) in enumerate(srcs):
            S.dma("sp", sl(ap), src, [b_wqd[wkey]], [wp4_buf[i]], append=(k > 0))
        return ap, wp4_buf[i]

    def phaseII(b, l):
        last = (l == NL - 1)
        S.mark("PII b%d l%d" % (b, l))
        groups = GROUPS if not last else GROUPS[1:]
        R3.reset()
        R3.off = 16384
        R4.reset()
        r3o = R3.base + R3.off
        yT, b_yT = R3.alloc("yT", [4, 16, 128], BF16)
        gT_extra, _ = R3.alloc("gTx", [6, 512], BF16)
        gT = arena[:, r3o // 2:r3o // 2 + 22 * 512].rearrange("p (a b) -> p a b", a=22)
        b_yTp = [Buf("yT_p%d" % i) for i in range(3)]
        b_gTl = b_yTp
        mT, b_mT = R3.alloc("mT", [8, 512], BF16)
        xt, _ = R3.alloc("xt", [4, D], F32)
        b_xt = [Buf("xt%d" % i) for i in range(4)]
        h2T, b_h2T = R3.alloc("h2T", [8, 512], BF16)
        g1_bc, b_g1 = R4.alloc("g1_bc", [D], F32)
        g2_bc, b_g2 = R4.alloc("g2_bc", [D], F32)
        sigs = [R4.alloc("sig%d" % i, [512], F32) for i in range(2)]
        tmpm, b_tmpm = R4.alloc("tmpm", [512], F32)
        macc, b_macc = R4.alloc("macc", [512], F32)
        tmpx, b_tmpx = R4.alloc("tmpx", [D], F32)
        xns = [R4.alloc("xn%d" % i, [D], BF16) for i in range(2)]
        accas = [R4.alloc("acca%d" % i, [512], F32) for i in range(2)]
        accvs = [R4.alloc("accv%d" % i, [512], F32) for i in range(2)]
        sas = [R4.alloc("sa%d" % i, [512], F32) for i in range(1)] * 2
        cur_bi = [None]
        k_sig = [0]
        k_pair = 0
        pendE = [None]

        def merge_w(jb):
            wg = wl4([24, 256], [((lambda a: a), wq_g[l, jb].rearrange("p (k w) -> p k w", k=24))], ("in", l))
            wbr = wl4([16, 256], [((lambda a: a), wq_mo[l, jb].rearrange("p (k w) -> p k w", k=16))], ("mo", l))
            return wg, wbr

        for (t0, ntl) in groups:
            n = ntl * 128
            c0 = tcol(t0)
            bi = NB if t0 == 0 else b
            if cur_bi[0] != bi:
                cur_bi[0] = bi
                S.dma("sp", g1_bc, modrow_d[l, bi, 2 * D:3 * D].partition_broadcast(128), [b_modrow], [b_g1])
                S.dma("sp", g2_bc, modrow_d[l, bi, 5 * D:6 * D].partition_broadcast(128), [b_modrow], [b_g2])
            mw0 = merge_w(0)
            S.dma("sp", yT[:, 0:ntl, 0:8, :], ys_d[b, t0:t0 + ntl].rearrange("t p (c n) -> p t c n", c=8), [b_ysd], b_yTp)
            S.dma("sp", yT[:, 0:ntl, 8:12, :], yf_d[b, t0:t0 + ntl].rearrange("t p (c n) -> p t c n", c=4), [b_yfd], [b_yTp[1]], append=True)
            S.dma("sp", yT[:, 0:ntl, 12:16, :], yg_d[b, t0:t0 + ntl].rearrange("t p (c n) -> p t c n", c=4), [b_ygd], [b_yTp[2]], append=True)

            S.mark("IIa g%d b%d l%d" % (t0, b, l))

            def gen_A(t0=t0, ntl=ntl, n=n, c0=c0, mw=mw0):
                for jb in range(4):
                    (wg, wgb), (wbr, wbb) = mw
                    if jb < 3:
                        mw = merge_w(jb + 1)
                    for jj in range(2):
                        j = jb * 2 + jj
                        krange = [(0, 8), (8, 12), (12, 16)]
                        for br in range(3):
                            pg, pgb = next_bank()
                            acc(pg[:, 0:n], [pgb], [(wg[:, br * 8 + kc, jj * 128:(jj + 1) * 128], hT[:, kc, c0:c0 + n], [wgb] + b_hT[t0:t0 + ntl]) for kc in range(8)])
                            sig, b_sig = sigs[k_sig[0] % 2]
                            k_sig[0] += 1
                            S.op("act", lambda e: e.activation(out=sig[:, 0:n], in_=pg[:, 0:n], func=AF.Sigmoid), [pgb], [b_sig])
                            pp, ppb = next_bank()
                            k0, k1 = krange[br]
                            acc(pp[:, 0:n], [ppb], [(wbr[:, kc, jj * 128:(jj + 1) * 128], yT[:, 0:ntl, kc, :], [wbb, b_yTp[br]]) for kc in range(k0, k1)])
                            if br == 0:
                                S.op("dve", lambda e: e.tensor_tensor(out=macc[:, 0:n], in0=pp[:, 0:n], in1=sig[:, 0:n], op=ALU.mult), [ppb, b_sig], [b_macc])
                            else:
                                S.op("dve", lambda e: e.tensor_tensor(out=tmpm[:, 0:n], in0=pp[:, 0:n], in1=sig[:, 0:n], op=ALU.mult), [ppb, b_sig], [b_tmpm])
                                if br == 1:
                                    S.op("dve", lambda e: e.tensor_tensor(out=macc[:, 0:n], in0=macc[:, 0:n], in1=tmpm[:, 0:n], op=ALU.add), [b_macc, b_tmpm], [b_macc])
                                else:
                                    S.op("dve", lambda e: e.tensor_tensor(out=mT[:, j, 0:n], in0=macc[:, 0:n], in1=tmpm[:, 0:n], op=ALU.add), [b_macc, b_tmpm], [b_mT])
                            yield

            ga = gen_A()
            if pendE[0] is not None:
                ge = pendE[0]
                pendE[0] = None
                edone = False
                while not edone:
                    try:
                        next(ge)
                    except StopIteration:
                        edone = True
                    for _ in range(3):
                        next(ga, None)
            for i in range(ntl):
                t = t0 + i
                if l == 0:
                    src = xrow(b, t)
                    rb = []
                else:
                    src = xres_d[b, (t - 2) * 128:(t - 1) * 128, :]
                    rb = [b_xres]
                S.dma("sp", xt[:, i, :], src, rb, [b_xt[i]])
            for _ in ga:
                pass
            S.mark("IIb g%d b%d l%d" % (t0, b, l))
            wos = [wl4([8, 512], [((lambda a: a), wq_out[l, hf].rearrange("p (k w) -> p k w", k=8))], ("out", l)) for hf in range(2)]
            wus = {}

            def up_w(pb6):
                ncol = (4 if pb6 < 5 else 2) * 128
                wa = wl4([8, ncol], [((lambda a: a), wq_ua[l, pb6][:, 0:8 * ncol].rearrange("p (k w) -> p k w", k=8))], ("up", l))
                wv = wl4([8, ncol], [((lambda a: a), wq_uvv[l, pb6][:, 0:8 * ncol].rearrange("p (k w) -> p k w", k=8))], ("up", l))
                return wa, wv

            wus[0] = up_w(0)
            def wout_mm(i):
                po, pob = next_dbl()
                for hf in range(2):
                    acc(po[:, hf * 512:(hf + 1) * 512], [pob[hf]], [(mT[:, kc, i * 128:(i + 1) * 128], wos[hf][0][:, kc, :], [wos[hf][1], b_mT]) for kc in range(8)])
                return po, pob

            def resid_ss(i, pend):
                po, pob = pend
                S.op("dve", lambda e: e.tensor_tensor(out=tmpx, in0=po, in1=g1_bc, op=ALU.mult), pob + [b_g1], [b_tmpx])
                S.op("dve", lambda e: e.tensor_tensor(out=xt[:, i, :], in0=xt[:, i, :], in1=tmpx, op=ALU.add), [b_xt[i], b_tmpx], [b_xt[i]])
                norm_ss(xt[:, i, :], [b_xt[i]], xns[i % 2][0], xns[i % 2][1], i % 2)

            pend = wout_mm(0)
            nxt = wout_mm(1) if ntl > 1 else None
            resid_ss(0, pend)
            for i in range(ntl):
                if i + 1 < ntl:
                    pend = nxt
                    nxt = wout_mm(i + 2) if i + 2 < ntl else None
                    resid_ss(i + 1, pend)
                norm_xn(xt[:, i, :], [b_xt[i]], xns[i % 2][0], xns[i % 2][1], i % 2)
                norm_back(xns[i % 2][0], xns[i % 2][1], l, bi, 2, h2T, [b_h2T], i * 128)
            S.mark("IIc g%d b%d l%d" % (t0, b, l))
            if t0 == 0:
                R_, W_ = 1, 256
            else:
                R_, W_ = ntl * 2, 64
            for pb6 in range(6):
                npair = 4 if pb6 < 5 else 2
                (wa, wab), (wv, wvb) = wus[pb6]
                if pb6 < 5:
                    wus[pb6 + 1] = up_w(pb6 + 1)
                for pp_ in range(npair):
                    p = pb6 * 4 + pp_
                    pa_, pab = next_bank()
                    acc(pa_[:, 0:n], [pab], [(wa[:, kc, pp_ * 128:(pp_ + 1) * 128], h2T[:, kc, 0:n], [wab, b_h2T]) for kc in range(8)])
                    pv_, pvb = next_bank()
                    acc(pv_[:, 0:n], [pvb], [(wv[:, kc, pp_ * 128:(pp_ + 1) * 128], h2T[:, kc, 0:n], [wvb, b_h2T]) for kc in range(8)])
                    acca, b_acca = accas[k_pair % 2]
                    accv, b_accv = accvs[k_pair % 2]
                    sa_, b_sa = sas[k_pair % 2]
                    k_pair += 1
                    for (ps_, psb, ac_, bb_a, ch) in ((pa_, pab, acca, b_acca, p), (pv_, pvb, accv, b_accv, 22 + p)):
                        S.op("act", lambda e: e.activation(out=ac_[:, 0:n], in_=ps_[:, 0:n], func=AF.Identity, scale=fcwT[:, l, ch, 1:2], bias=fcbT[:, l, ch:ch + 1]),
                             [psb, b_const], [bb_a])
                        a3 = ac_[:, 0:n].rearrange("p (r w) -> p r w", r=R_)
                        p3 = ps_[:, 0:n].rearrange("p (r w) -> p r w", r=R_)
                        S.op("dve", lambda e: e.scalar_tensor_tensor(out=a3[:, :, 1:W_], in0=p3[:, :, 0:W_ - 1], scalar=fcwT[:, l, ch, 0:1], in1=a3[:, :, 1:W_],
                                                                      op0=ALU.mult, op1=ALU.add), [psb, b_const, bb_a], [bb_a])
                        S.op("dve", lambda e: e.scalar_tensor_tensor(out=a3[:, :, 0:W_ - 1], in0=p3[:, :, 1:W_], scalar=fcwT[:, l, ch, 2:3], in1=a3[:, :, 0:W_ - 1],
                                                                      op0=ALU.mult, op1=ALU.add), [psb, b_const, bb_a], [bb_a])
                    S.op("act", lambda e: e.activation(out=sa_[:, 0:n], in_=acca[:, 0:n], func=AF.Silu), [b_acca], [b_sa])
                    S.op("pool", lambda e: e.tensor_tensor(out=gT[:, p, 0:n], in0=accv[:, 0:n], in1=sa_[:, 0:n], op=ALU.mult), [b_accv, b_sa], b_gTl)
            S.mark("IId g%d b%d l%d" % (t0, b, l))
            wds = {0: wl4([22, 256], [((lambda a: a), wq_dn[l, 0].rearrange("p (k w) -> p k w", k=22))], ("dn", l))}
            for qt in range(4):
                wd, wdb = wds[qt]
                if qt < 3:
                    wds[qt + 1] = wl4([22, 256], [((lambda a: a), wq_dn[l, qt + 1].rearrange("p (k w) -> p k w", k=22))], ("dn", l))
                for i in range(ntl):
                    pd_, pdb = next_bank()
                    acc(pd_[:, 0:256], [pdb], [(gT[:, p, i * 128:(i + 1) * 128], wd[:, p, :], [wdb] + b_gTl) for p in range(22)])
                    S.op("dve", lambda e: e.tensor_tensor(out=tmpx[:, 0:256], in0=pd_[:, 0:256], in1=g2_bc[:, qt * 256:(qt + 1) * 256], op=ALU.mult), [pdb, b_g2], [b_tmpx])
                    S.op("dve", lambda e: e.tensor_tensor(out=xt[:, i, qt * 256:(qt + 1) * 256], in0=xt[:, i, qt * 256:(qt + 1) * 256], in1=tmpx[:, 0:256], op=ALU.add),
                         [b_xt[i], b_tmpx], [b_xt[i]])
            S.mark("IIe g%d b%d l%d" % (t0, b, l))

            def gen_E(t0=t0, ntl=ntl, bi=bi):
                for i in range(ntl):
                    t = t0 + i
                    if not last:
                        if t >= 2:
                            S.dma("pool", xres_d[b, (t - 2) * 128:(t - 1) * 128, :], xt[:, i, :], [b_xt[i]], [b_xres], append=True)
                        norm_front(xt[:, i, :], [b_xt[i]], *xns[i % 2])
                        yield
                        if i >= 1:
                            norm_back(xns[(i - 1) % 2][0], xns[(i - 1) % 2][1], l + 1, bi, 1, hT, [b_hT[t - 1]], tcol(t - 1))
                            yield
                        if i == ntl - 1:
                            norm_back(xns[i % 2][0], xns[i % 2][1], l + 1, bi, 1, hT, [b_hT[t]], tcol(t))
                            yield
                    else:
                        junk, b_junk = xns[i % 2]
                        S.op("dve", lambda e: e.scalar_tensor_tensor(out=junk, in0=xt[:, i, :], scalar=1.0, in1=xt[:, i, :], op0=ALU.mult, op1=ALU.mult, accum_out=ss[:, 0:1]),
                             [b_xt[i]], [b_junk, b_ssk[0]], nsl=2)
                        rstd_from_ss(0, D)
                        S.op("dve", lambda e: e.scalar_tensor_tensor(out=tmpx, in0=xt[:, i, :], scalar=rstd[:, 0:1], in1=fw_bc, op0=ALU.mult, op1=ALU.mult),
                             [b_xt[i], b_rstdk[0], b_const], [b_tmpx])
                        S.dma("pool", out_d[b, (t - 2) * 128:(t - 1) * 128, :], tmpx, [b_tmpx], [b_out], append=True)
                        yield

            pendE[0] = gen_E()
        if pendE[0] is not None:
            for _ in pendE[0]:
                pass
            pendE[0] = None
        S.barrier()

    for b in range(NB):
        phase1(b)
        for l in range(NL):
            phaseI(b, l)
            phaseII(b, l)
    S.barrier()
    S.mark("end")
    return nc, S


def _consts():
    bf = ml_dtypes.bfloat16
    k = np.arange(128)
    c = {}
    c["ident"] = np.eye(128, dtype=np.float32).astype(bf)
    c["trif"] = (k[:, None] <= k[None, :]).astype(np.float32)
    c["trib"] = (k[:, None] >= k[None, :]).astype(np.float32)
    c["ones"] = np.ones((128, 128), np.float32)
    sel3 = np.zeros((96, 16, 128), np.float32)
    for h in range(16):
        for r in (0, 32, 64):
            sel3[r + h, h, :] = 1.0
    c["sel3"] = sel3.astype(bf)
    c["identF"] = np.eye(128, dtype=np.float32)
    negm = np.zeros((128, 2, 128), np.float32)
    negm[:, 0, :] = np.where(k[:, None] > k[None, :], -30000.0, 0.0)
    negm[:, 1, :] = np.where(k[:, None] < k[None, :], -30000.0, 0.0)
    c["negm"] = negm.astype(bf)
    ang = 2 * np.pi * ((k[:, None] * k[None, :]) % 128) / 128.0
    c["cs"] = (np.concatenate([np.cos(ang), np.sin(ang)], axis=1) / np.sqrt(128.0)).astype(bf)
    for L, sfx, kw in ((SEQ, "", 512), (CTXL, "256", 256)):
        t = np.arange(L, dtype=np.int64)
        a = 2 * np.pi * ((t[:, None] * t[None, :]) % L).astype(np.float64) / L
        tab = np.stack([np.cos(a), -np.sin(a)], axis=0) / np.sqrt(L)
        nkb, ntt2 = L // kw, L // 256
        tab = tab.reshape(2, ntt2, 2, 128, nkb, kw)
        tab = np.transpose(tab, (4, 1, 3, 2, 0, 5))
        c["cst" + sfx] = np.ascontiguousarray(tab.reshape(nkb, ntt2, 128, 4 * kw)).astype(np.float32).astype(bf)
    return c


_CACHE = {}


def _pT(v):
    v = np.asarray(v, np.float32)
    lead = v.shape[:-1]
    n = v.shape[-1] // 128
    r = v.reshape(lead + (n, 128))
    return np.ascontiguousarray(np.moveaxis(r, -1, 0))


def kernel(**inp):
    NB = 32 // NCORES
    if "nc" not in _CACHE:
        _CACHE["nc"] = build(NB)[0]
        _CACHE["consts"] = _consts()
    nc = _CACHE["nc"]
    f = lambda a: np.ascontiguousarray(np.asarray(a, np.float32))
    shared = dict(_CACHE["consts"])
    NL = 2
    shared["w_mod"] = f(inp["w_mod"])
    shared["b_mod"] = f(inp["b_mod"])
    shared["bmodT"] = _pT(inp["b_mod"])
    shared["n1T"] = _pT(inp["norm1_w"])
    shared["n2T"] = _pT(inp["norm2_w"])
    shared["final_norm_w"] = f(inp["final_norm_w"])
    shared["w_in"] = f(inp["w_in"])
    cw = np.asarray(inp["ssd_conv_w"], np.float32)
    shared["cwT"] = np.ascontiguousarray(np.transpose(cw.reshape(NL, 5, 12, 128), (3, 0, 2, 1)))
    shared["cbT"] = _pT(inp["ssd_conv_b"])
    shared["ssd_a_log"] = f(inp["ssd_a_log"]).reshape(NL, 32)
    shared["ssd_dt_bias"] = f(inp["ssd_dt_bias"]).reshape(NL, 32)
    shared["ssd_d"] = f(inp["ssd_d"])
    shared["snT"] = _pT(inp["ssd_norm_w"])
    ws = np.asarray(inp["gmlp_w_s"], np.float32)
    shared["wsT"] = np.ascontiguousarray(np.transpose(ws, (0, 3, 1, 2)))
    shared["gmlp_b_s"] = f(inp["gmlp_b_s"]).reshape(NL, 512)
    shared["w_ssd_o"] = f(inp["w_ssd_o"])
    shared["w_fft_o"] = f(inp["w_fft_o"])
    shared["w_gmlp_o"] = f(inp["w_gmlp_o"])
    shared["w_out"] = f(inp["w_out"])
    shared["ffn_w_up"] = f(inp["ffn_w_up"])
    fw = np.asarray(inp["ffn_conv_w"], np.float32)
    shared["fcwT"] = np.ascontiguousarray(np.transpose(fw.reshape(NL, 3, 44, 128), (3, 0, 2, 1)))
    shared["fcbT"] = _pT(inp["ffn_conv_b"])
    shared["ffn_w_down"] = f(inp["ffn_w_down"])
    x = f(inp["x"])
    ctx = f(inp["ctx"])
    c = f(inp["c"])
    cctx = f(inp["c_ctx"])
    in_maps = []
    for i in range(NCORES):
        m = dict(shared)
        m["x"] = x[i * NB:(i + 1) * NB]
        m["ctx"] = ctx[i * NB:(i + 1) * NB]
        cc = np.concatenate([c[i * NB:(i + 1) * NB], cctx[None, :]], axis=0)
        m["ccT"] = np.ascontiguousarray(np.transpose(cc.reshape(NB + 1, 8, 128), (2, 1, 0)))
        in_maps.append(m)
    res = run_bass_kernel_spmd(nc, in_maps, core_ids=list(range(NCORES)))
    return np.concatenate([r["out"] for r in res.results], axis=0).astype(np.float32)
```
